# Optimizing a Trainium2 kernel written in Bass

```python
import math
import jax, jax.numpy as jnp
from jax import lax
import numpy as np

D_MODEL = 4096
BATCH = 2
SEQ = 8192
DEPTH = 1

GRID_W = 64
CTX_LEN = 256
D_MIX = D_MODEL
HY_WIDTH = D_MIX // 2
HY_GROUP = 128
HY_ORDER = 2
HY_POS_BANDS = 16
HY_POS_DIM = 1 + 2 * HY_POS_BANDS
HY_FILT_HIDDEN = 64
HY_DECAY_TARGET = 1e-2
HY_FAST_DECAY_PCT = 0.3
HY_SLOW_DECAY_PCT = 1.5
MLA_NOPE = 128
MLA_ROPE = 64
MLA_QK = MLA_NOPE + MLA_ROPE
MLA_V = 128
MLA_HEADS = (D_MIX - HY_WIDTH) // MLA_V
Q_LORA = 1024
KV_LORA = 512
ROPE_THETA = 10000.0
D_FF = 4 * D_MODEL
Q_BLOCK = 128
NORM_EPS = 1e-6
N_MOD = 6
HY_COLS = 3 * HY_WIDTH
IN_COLS = HY_COLS + Q_LORA + KV_LORA + MLA_ROPE

kernel_name = 'hyena_mla_parallel_heads_dit_block'


def rms_norm(x, g):
    xf = x.astype(jnp.float32)
    y = xf * lax.rsqrt(jnp.mean(xf * xf, axis=-1, keepdims=True) + NORM_EPS)
    return (y * g.astype(jnp.float32)).astype(x.dtype)


def modulate(x, g, shift, scale):
    return rms_norm(x, g) * (1.0 + scale) + shift


def adaln_chunks(cvec, w, b, n):
    m = jax.nn.silu(cvec) @ w[:, :n * D_MODEL] + b[:n * D_MODEL]
    return jnp.split(m, n, axis=-1)


def short_conv_centred(u, w, b):
    L = u.shape[1]
    up = jnp.pad(u, ((0, 0), (1, 1), (0, 0)))
    return up[:, :L] * w[0] + up[:, 1:L + 1] * w[1] + up[:, 2:] * w[2] + b


def hyena_filters(L, w1, b1, w2, b2, w3, freq):
    f32 = jnp.float32
    pos = jnp.arange(L, dtype=f32)
    t = jnp.linspace(0.0, 1.0, L, dtype=f32)[:, None]
    bands = jnp.linspace(1e-4, HY_POS_BANDS - 1, HY_POS_BANDS, dtype=f32)
    ang = (2.0 * math.pi / L) * pos[:, None] * bands[None, :]
    feats = jnp.concatenate([t, jnp.cos(ang), -jnp.sin(ang)], axis=-1)
    fr = freq.astype(f32)
    h = jnp.sin(fr * (feats @ w1.astype(f32) + b1.astype(f32)))
    h = jnp.sin(fr * (h @ w2.astype(f32) + b2.astype(f32)))
    filt = (h @ w3.astype(f32)).reshape(L, 2, HY_ORDER, HY_WIDTH)
    deltas = jnp.abs(jnp.linspace(math.log(HY_DECAY_TARGET) / HY_SLOW_DECAY_PCT,
                                  math.log(HY_DECAY_TARGET) / HY_FAST_DECAY_PCT, HY_WIDTH, dtype=f32))
    decay = jnp.exp(-t * deltas[None, :])
    return filt * decay[:, None, None, :]


def bidir_long_conv(u, h_fwd, h_bwd, bias):
    L = u.shape[1]
    k = jnp.concatenate([h_fwd, jnp.zeros_like(h_fwd[:1]), h_bwd[:0:-1]], axis=0)
    k_f = jnp.fft.rfft(k, n=2 * L, axis=0)
    u_f = jnp.fft.rfft(u.astype(jnp.float32), n=2 * L, axis=1)
    y = jnp.fft.irfft(u_f * k_f[None], n=2 * L, axis=1)[:, :L]
    return (y + u.astype(jnp.float32) * bias.astype(jnp.float32)).astype(u.dtype)


def hyena_mixer(hy_in, conv_w, conv_b, filt, hy_bias):
    u = short_conv_centred(hy_in, conv_w, conv_b)
    v, x1, x2 = jnp.split(u, 3, axis=-1)
    z = x1 * bidir_long_conv(v, filt[:, 0, 0], filt[:, 1, 0], hy_bias[0])
    return x2 * bidir_long_conv(z, filt[:, 0, 1], filt[:, 1, 1], hy_bias[1])


def axial_rope_tables(L):
    rows = L // GRID_W
    row = jnp.repeat(jnp.arange(rows, dtype=jnp.float32), GRID_W)
    col = jnp.tile(jnp.arange(GRID_W, dtype=jnp.float32), rows)
    half = MLA_ROPE // 2
    inv = ROPE_THETA ** (-jnp.arange(0, half, 2, dtype=jnp.float32) / half)
    ang = jnp.concatenate([row[:, None] * inv, col[:, None] * inv], axis=-1)
    return jnp.cos(ang), jnp.sin(ang)


def apply_axial_rope(x, cos, sin):
    B, L, H, _ = x.shape
    nf = MLA_ROPE // 4
    xr = x.reshape(B, L, H, 2, 2, nf)
    x1, x2 = xr[..., 0, :], xr[..., 1, :]
    c = cos.reshape(L, 2, nf)[None, :, None]
    s = sin.reshape(L, 2, nf)[None, :, None]
    out = jnp.stack([x1 * c - x2 * s, x1 * s + x2 * c], axis=-2)
    return out.reshape(B, L, H, MLA_ROPE).astype(x.dtype)


def mla_queries(q_a, g_qa, w_qb, q_norm_g, rope):
    B, L, _ = q_a.shape
    q = (rms_norm(q_a, g_qa) @ w_qb).reshape(B, L, MLA_HEADS, MLA_QK)
    q = rms_norm(q, q_norm_g)
    if rope is not None:
        q = jnp.concatenate([q[..., :MLA_NOPE], apply_axial_rope(q[..., MLA_NOPE:], *rope)], axis=-1)
    return q


def mla_keys_values(kv_a, k_rope, g_kva, w_kvb, k_norm_g, rope):
    B, L, _ = kv_a.shape
    kv = (rms_norm(kv_a, g_kva) @ w_kvb).reshape(B, L, MLA_HEADS, MLA_NOPE + MLA_V)
    k_nope, v = kv[..., :MLA_NOPE], kv[..., MLA_NOPE:]
    k_r = jnp.broadcast_to(k_rope[:, :, None, :], (B, L, MLA_HEADS, MLA_ROPE)).astype(k_nope.dtype)
    k = rms_norm(jnp.concatenate([k_nope, k_r], axis=-1), k_norm_g)
    if rope is not None:
        k = jnp.concatenate([k[..., :MLA_NOPE], apply_axial_rope(k[..., MLA_NOPE:], *rope)], axis=-1)
    return k, v


def attend_latent(q, k_lat, v_lat, k_ctx, v_ctx):
    B, L, H, _ = q.shape
    k = jnp.concatenate([k_ctx, k_lat], axis=1)
    v = jnp.concatenate([v_ctx, v_lat], axis=1)
    nblk = L // Q_BLOCK
    qb = q.reshape(B, nblk, Q_BLOCK, H, MLA_QK).transpose(1, 0, 2, 3, 4)
    scale = MLA_QK ** -0.5

    def one_block(qi):
        s = jnp.einsum('bqhd,bkhd->bhqk', qi, k, preferred_element_type=jnp.float32) * scale
        p = jax.nn.softmax(s, axis=-1)
        return jnp.einsum('bhqk,bkhd->bqhd', p.astype(v.dtype), v)

    o = lax.map(one_block, qb)
    return o.transpose(1, 0, 2, 3, 4).reshape(B, L, H * MLA_V)


def attend_ctx(q, k, v):
    B, Lc, H, _ = q.shape
    s = jnp.einsum('bqhd,bkhd->bhqk', q, k, preferred_element_type=jnp.float32) * (MLA_QK ** -0.5)
    p = jax.nn.softmax(s, axis=-1)
    return jnp.einsum('bhqk,bkhd->bqhd', p.astype(v.dtype), v).reshape(B, Lc, H * MLA_V)


def sq_relu_mlp(h, w1, w2):
    return jnp.square(jax.nn.relu(h @ w1)) @ w2


def setup_inputs(seed: int = 0) -> dict:
    key = jax.random.key(seed)
    ks = jax.random.split(key, 32)
    f32 = jnp.float32

    def nrm(k, shape, scale):
        return jax.random.normal(k, shape, dtype=f32) * scale

    def gain(k, shape):
        return 1.0 + 0.01 * jax.random.normal(k, shape, dtype=f32)

    return {
        'x': nrm(ks[0], (BATCH, SEQ, D_MODEL), 1.0),
        'c': nrm(ks[1], (BATCH, D_MODEL), 1.0),
        'ctx': nrm(ks[2], (BATCH, CTX_LEN, D_MODEL), 1.0),
        'c_ctx': nrm(ks[3], (D_MODEL,), 1.0),
        'norm1_g': gain(ks[4], (DEPTH, D_MODEL)),
        'norm2_g': gain(ks[5], (DEPTH, D_MODEL)),
        'w_ada': nrm(ks[6], (DEPTH, D_MODEL, N_MOD * D_MODEL), D_MODEL ** -0.5),
        'b_ada': nrm(ks[7], (DEPTH, N_MOD * D_MODEL), 0.01),
        'w_in': nrm(ks[8], (DEPTH, D_MODEL, IN_COLS), D_MODEL ** -0.5),
        'hy_conv_w': nrm(ks[9], (DEPTH, 3, HY_COLS), 3 ** -0.5),
        'hy_conv_b': nrm(ks[10], (DEPTH, HY_COLS), 0.01),
        'hy_filt_w1': nrm(ks[11], (DEPTH, HY_POS_DIM, HY_FILT_HIDDEN), HY_POS_DIM ** -0.5),
        'hy_filt_b1': nrm(ks[12], (DEPTH, HY_FILT_HIDDEN), 0.1),
        'hy_filt_w2': nrm(ks[13], (DEPTH, HY_FILT_HIDDEN, HY_FILT_HIDDEN), HY_FILT_HIDDEN ** -0.5),
        'hy_filt_b2': nrm(ks[14], (DEPTH, HY_FILT_HIDDEN), 0.1),
        'hy_filt_w3': nrm(ks[15], (DEPTH, HY_FILT_HIDDEN, 2 * HY_ORDER * HY_WIDTH), 0.004),
        'hy_freq': gain(ks[16], (DEPTH, HY_FILT_HIDDEN)),
        'hy_bias': nrm(ks[17], (DEPTH, HY_ORDER, HY_WIDTH), 0.1),
        'mla_g_qa': gain(ks[18], (DEPTH, Q_LORA)),
        'mla_w_qb': nrm(ks[19], (DEPTH, Q_LORA, MLA_HEADS * MLA_QK), Q_LORA ** -0.5),
        'mla_g_kva': gain(ks[20], (DEPTH, KV_LORA)),
        'mla_w_kvb': nrm(ks[21], (DEPTH, KV_LORA, MLA_HEADS * (MLA_NOPE + MLA_V)), KV_LORA ** -0.5),
        'mla_q_norm_g': gain(ks[22], (DEPTH, MLA_QK)),
        'mla_k_norm_g': gain(ks[23], (DEPTH, MLA_QK)),
        'w_out': nrm(ks[24], (DEPTH, D_MIX, D_MODEL), D_MIX ** -0.5),
        'w_mlp1': nrm(ks[25], (DEPTH, D_MODEL, D_FF), D_MODEL ** -0.5),
        'w_mlp2': nrm(ks[26], (DEPTH, D_FF, D_MODEL), D_FF ** -0.5),
    }


def reference(x, c, ctx, c_ctx, norm1_g, norm2_g, w_ada, b_ada, w_in, hy_conv_w, hy_conv_b,
              hy_filt_w1, hy_filt_b1, hy_filt_w2, hy_filt_b2, hy_filt_w3, hy_freq, hy_bias,
              mla_g_qa, mla_w_qb, mla_g_kva, mla_w_kvb, mla_q_norm_g, mla_k_norm_g,
              w_out, w_mlp1, w_mlp2):
    L = x.shape[1]
    Lc = ctx.shape[1]
    rope = axial_rope_tables(L)
    q_end = HY_COLS + Q_LORA
    kv_end = q_end + KV_LORA

    for l in range(DEPTH):
        last = l == DEPTH - 1
        sh1, sc1, g1, sh2, sc2, g2 = [m[:, None, :] for m in adaln_chunks(c, w_ada[l], b_ada[l], N_MOD)]
        n_ctx_mod = 2 if last else N_MOD
        ctx_mod = adaln_chunks(c_ctx, w_ada[l], b_ada[l], n_ctx_mod)

        hc = modulate(ctx, norm1_g[l], ctx_mod[0], ctx_mod[1])
        if last:
            proj_c = hc @ w_in[l][:, HY_COLS:]
            kv_a_c = proj_c[..., Q_LORA:Q_LORA + KV_LORA]
            k_rope_c = proj_c[..., Q_LORA + KV_LORA:]
        else:
            proj_c = hc @ w_in[l]
            kv_a_c = proj_c[..., q_end:kv_end]
            k_rope_c = proj_c[..., kv_end:]
        k_c, v_c = mla_keys_values(kv_a_c, k_rope_c, mla_g_kva[l], mla_w_kvb[l], mla_k_norm_g[l], None)

        h = modulate(x, norm1_g[l], sh1, sc1)
        proj = h @ w_in[l]
        filt = hyena_filters(L, hy_filt_w1[l], hy_filt_b1[l], hy_filt_w2[l], hy_filt_b2[l],
                             hy_filt_w3[l], hy_freq[l])
        y_hy = hyena_mixer(proj[..., :HY_COLS], hy_conv_w[l], hy_conv_b[l], filt, hy_bias[l])
        q = mla_queries(proj[..., HY_COLS:q_end], mla_g_qa[l], mla_w_qb[l], mla_q_norm_g[l], rope)
        k, v = mla_keys_values(proj[..., q_end:kv_end], proj[..., kv_end:], mla_g_kva[l], mla_w_kvb[l],
                               mla_k_norm_g[l], rope)
        y_att = attend_latent(q, k, v, k_c, v_c)
        x_new = x + g1 * (jnp.concatenate([y_hy, y_att], axis=-1) @ w_out[l])
        x_new = x_new + g2 * sq_relu_mlp(modulate(x_new, norm2_g[l], sh2, sc2), w_mlp1[l], w_mlp2[l])

        if not last:
            csh2, csc2, cg1, cg2 = ctx_mod[3], ctx_mod[4], ctx_mod[2], ctx_mod[5]
            filt_c = hyena_filters(Lc, hy_filt_w1[l], hy_filt_b1[l], hy_filt_w2[l], hy_filt_b2[l],
                                   hy_filt_w3[l], hy_freq[l])
            yc_hy = hyena_mixer(proj_c[..., :HY_COLS], hy_conv_w[l], hy_conv_b[l], filt_c, hy_bias[l])
            q_c = mla_queries(proj_c[..., HY_COLS:q_end], mla_g_qa[l], mla_w_qb[l], mla_q_norm_g[l], None)
            yc_att = attend_ctx(q_c, k_c, v_c)
            ctx_new = ctx + cg1 * (jnp.concatenate([yc_hy, yc_att], axis=-1) @ w_out[l])
            ctx = ctx_new + cg2 * sq_relu_mlp(modulate(ctx_new, norm2_g[l], csh2, csc2), w_mlp1[l], w_mlp2[l])
        x = x_new
    return x
```

```python
import math
import numpy as np
import concourse.bass as bass
import concourse.mybir as mybir
from concourse.bass_utils import run_bass_kernel_spmd

F32 = mybir.dt.float32
BF16 = mybir.dt.bfloat16
AF = mybir.ActivationFunctionType
ALU = mybir.AluOpType

D = 4096
L = 8192
LC = 256
NB = 2
HYW = 2048
HYC = 6144
QL = 1024
KVL = 512
ROPE = 64
NOPE = 128
QK = 192
DV = 128
NH = 16
DFF = 16384
INC = 7744
EPS = 1e-6
NFFT = 16384
PI = math.pi


class Buf:
    __slots__ = ("w", "r", "name")

    def __init__(self, name=""):
        self.w = None
        self.r = {}
        self.name = name


class Eng:
    def __init__(self, name, handle, sem, is_pe=False):
        self.name = name
        self.h = handle
        self.sem = sem
        self.count = 0
        self.ninst = 0
        self.waited = {}
        self.is_pe = is_pe


class K:
    def __init__(self, nc, sems, n_sp=20, n_pool=12):
        self.nc = nc
        it = iter(sems)
        self.pe = Eng("pe", nc.tensor, next(it), True)
        self.act = Eng("act", nc.scalar, next(it))
        self.dve = Eng("dve", nc.vector, next(it))
        self.pool = Eng("pool", nc.gpsimd, next(it))
        self.sp = Eng("sp", nc.sync, None)
        self.q = {
            "sp": dict(eng=self.sp, sems=[next(it) for _ in range(n_sp)], tot=[0] * n_sp, i=0),
            "pool": dict(eng=self.pool, sems=[next(it) for _ in range(n_pool)], tot=[0] * n_pool, i=0),
        }
        self.out_events = []

    def _wait(self, E, ev):
        if ev is None:
            return
        sem, val, src = ev
        if src is E and E.is_pe:
            return
        if E.waited.get(id(sem), 0) >= val:
            return
        E.h.wait_ge(sem, val)
        E.waited[id(sem)] = val
        E.ninst += 1

    def _deps(self, E, reads, writes):
        for b in reads:
            self._wait(E, b.w)
        for b in writes:
            self._wait(E, b.w)
            for ev in list(b.r.values()):
                self._wait(E, ev)

    def _mark(self, ev, key, reads, writes):
        for b in reads:
            b.r[key] = ev
        for b in writes:
            b.w = ev
            b.r = {}

    def op(self, E, fn, reads=(), writes=(), inc=True):
        self._deps(E, reads, writes)
        ins = fn(E.h)
        E.ninst += 1
        val = E.count + 1
        if inc:
            ins.then_inc(E.sem, 1)
            E.count = val
        self._mark((E.sem, val, E), id(E.sem), reads, writes)
        return ins

    def dma(self, qn, out, in_, reads=(), writes=(), is_output=False):
        Q = self.q[qn]
        E = Q["eng"]
        i = Q["i"]
        Q["i"] = (i + 1) % len(Q["sems"])
        sem = Q["sems"][i]
        if Q["tot"][i] > 0:
            self._wait(E, (sem, Q["tot"][i], None))
        self._deps(E, reads, writes)
        ins = E.h.dma_start(out=out, in_=in_)
        E.ninst += 1
        Q["tot"][i] += 16
        ins.then_inc(sem, 16)
        ev = (sem, Q["tot"][i], None)
        self._mark(ev, id(sem), reads, writes)
        if is_output:
            self.out_events.append(ev)
        return ins

    def barrier(self):
        engs = (self.pe, self.act, self.dve, self.pool, self.sp)
        for E in engs:
            for Q in self.q.values():
                for sem, tot in zip(Q["sems"], Q["tot"]):
                    if tot > 0:
                        self._wait(E, (sem, tot, None))
            for F in (self.pe, self.act, self.dve, self.pool):
                if F is not E and F.count > 0:
                    self._wait(E, (F.sem, F.count, F))

    def stats(self):
        return {E.name: E.ninst for E in (self.pe, self.act, self.dve, self.pool, self.sp)}

    def finish(self):
        for Q in self.q.values():
            for sem, tot in zip(Q["sems"], Q["tot"]):
                if tot > 0:
                    self._wait(self.sp, (sem, tot, None))
        for E in (self.pe, self.act, self.dve, self.pool):
            if E.count > 0:
                self._wait(self.sp, (E.sem, E.count, E))


def _consts(j):
    c = {}
    c["ident"] = np.eye(128, dtype=np.float32)
    n1 = np.arange(128)[:, None].astype(np.float64)
    k1 = np.arange(256)[None, :].astype(np.float64)
    ang = 2 * np.pi * n1 * k1 / 256.0
    c["f256"] = np.concatenate([np.cos(ang), -np.sin(ang)], 1).astype(np.float32)
    n2 = (np.arange(128) % 64)[:, None].astype(np.float64)
    th = 2 * np.pi * n2 * k1 / NFFT
    c["twc"] = np.cos(th).astype(np.float32)
    c["tws"] = np.sin(th).astype(np.float32)
    a = np.arange(64)[:, None] * np.arange(64)[None, :] * (2 * np.pi / 64.0)
    C = np.zeros((128, 128)); S = np.zeros((128, 128))
    for q in range(2):
        C[q * 64:(q + 1) * 64, q * 64:(q + 1) * 64] = np.cos(a)
        S[q * 64:(q + 1) * 64, q * 64:(q + 1) * 64] = np.sin(a)
    c["c64"] = C.astype(np.float32); c["s64"] = S.astype(np.float32); c["ns64"] = (-S).astype(np.float32)
    kk = np.arange(256)[:, None].astype(np.float64); nn = np.arange(128)[None, :].astype(np.float64)
    ph = 2 * np.pi * kk * nn / 256.0
    ic = (np.cos(ph) / NFFT).reshape(2, 128, 128); isn = (-np.sin(ph) / NFFT).reshape(2, 128, 128)
    c["icis"] = np.stack([ic, isn], 1).transpose(2, 0, 1, 3).reshape(128, 4 * 128).astype(np.float32)
    f32 = np.float32
    pos = np.arange(L, dtype=f32)
    t = np.linspace(0.0, 1.0, L, dtype=f32)
    bands = np.linspace(1e-4, 15, 16, dtype=f32)
    angp = (f32(2.0 * math.pi / L) * pos[:, None]) * bands[None, :]
    feats = np.concatenate([t[:, None], np.cos(angp), -np.sin(angp)], -1).astype(f32)
    c["featsT"] = np.ascontiguousarray(feats.T)
    c["negt"] = np.ascontiguousarray(-t.reshape(128, 64))
    deltas = np.abs(np.linspace(math.log(1e-2) / 1.5, math.log(1e-2) / 0.3, HYW, dtype=f32))
    c["deltas"] = deltas.reshape(1, HYW).astype(f32)
    half = ROPE // 2
    inv = (10000.0 ** (-np.arange(0, half, 2, dtype=f32) / half)).astype(f32)
    tok = np.arange(L)
    row = (tok // 64).astype(f32); col = (tok % 64).astype(f32)
    angr = np.concatenate([row[:, None] * inv, col[:, None] * inv], -1).astype(f32)
    cs, sn = np.cos(angr), np.sin(angr)
    idx = (64 * np.arange(128)[:, None] + np.arange(64)[None, :])
    c["ropek"] = np.concatenate([cs[idx], sn[idx]], -1).reshape(128, 64 * 64).astype(f32)
    own = 2048 * j + np.arange(2048).reshape(16, 128).T
    c["ropeq"] = np.concatenate([cs[own], sn[own]], -1).reshape(128, 16 * 64).astype(f32)
    sel = np.zeros((128, 32), np.float32)
    sel[32 * j + np.arange(32), np.arange(32)] = 1.0
    c["sel"] = sel
    c["ones"] = np.ones((128, 128), np.float32)
    c["zeros"] = np.zeros((1, 256), np.float32)
    return c


MAGIC = 12582912.0
NG_ALL = 32


def build_program(NG=NG_ALL, stop_after=None, dbg=False):
    from contextlib import ExitStack
    nc = bass.Bass("TRN2", target_bir_lowering=False)

    def din(name, shape, dt=F32):
        return nc.dram_tensor(name, list(shape), dt, kind="ExternalInput").ap()

    x = din("x", [L, D]); x_own = din("x_own", [2048, D]); ctx = din("ctx", [LC, D])
    cvec = din("cvec", [2, D]); norm_g = din("norm_g", [2, D])
    w_ada = din("w_ada", [D, 6 * D]); b_ada = din("b_ada", [1, 6 * D])
    w_hy = din("w_hy", [D, NG * 192]); w_q = din("w_q", [D, QL]); w_kv = din("w_kv", [D, 576])
    hy_cw = din("hy_cw", [3, NG * 192]); hy_cb = din("hy_cb", [1, NG * 192])
    f_w1 = din("f_w1", [33, 64]); f_b1 = din("f_b1", [64, 1]); f_w2 = din("f_w2", [64, 64]); f_b2 = din("f_b2", [64, 1])
    f_freq = din("f_freq", [64, 1]); f_w3 = din("f_w3", [64, NG * 256]); hy_bias = din("hy_bias", [1, NG * 128])
    g_qa = din("g_qa", [1, QL]); g_kva = din("g_kva", [1, KVL]); w_qb = din("w_qb", [QL, NH * QK]); w_kvb = din("w_kvb", [KVL, NH * 256])
    gq = din("gq", [1, QK]); gk = din("gk", [1, QK])
    tiny = stop_after is not None
    w_out = din("w_out", [D, D]); w_mlp1 = din("w_mlp1", [128, 128] if tiny else [D, DFF]); w_mlp2 = din("w_mlp2", [128, 128] if tiny else [DFF, D])
    cn = {}
    for nm, shp in [("ident", [128, 128]), ("f256", [128, 512]), ("twc", [128, 256]), ("tws", [128, 256]), ("c64", [128, 128]),
                    ("s64", [128, 128]), ("ns64", [128, 128]), ("icis", [128, 512]), ("featsT", [33, L]), ("negt", [128, 64]),
                    ("deltas", [1, NG * 64]), ("ropek", [128, 4096]), ("ropeq", [128, 1024]), ("sel", [128, 32]), ("ones", [128, 128]),
                    ("zeros", [1, 256])]:
        cn[nm] = din("c_" + nm, shp)
    out = nc.dram_tensor("out", [2048, D], F32, kind="ExternalOutput").ap()
    dbg_out = {}

    def dram(name, shape, dt):
        return nc.dram_tensor(name, list(shape), dt, kind="Internal").ap()

    mod_d = dram("mod_d", [2, 6 * D], F32)
    hT_d = dram("hT_d", [64, 128, D], BF16)
    yT_d = dram("yT_d", [D, 2048], BF16)
    xnew_d = dram("xnew_d", [2048, D], F32)
    B_mod, B_hT, B_yT, B_xnew = Buf("mod_d"), [Buf() for _ in range(64)], Buf("yT_d"), Buf("xnew_d")

    with ExitStack() as es0:
        sems = [es0.enter_context(nc.semaphore(f"s{i}")) for i in range(4 + 20 + 12)]
        k = K(nc, sems)
        pe, act, dve, pool = k.pe, k.act, k.dve, k.pool

        def sb(es, name, shape, dt=F32):
            return es.enter_context(nc.sbuf_tensor(name, list(shape), dt)), Buf(name)


        def cast_load(dst, src, B_dst, step=4):
            n = dst.shape[1]
            for a in range(0, n, step):
                k.dma("pool", dst[:, a:a + step, :], src[:, a:a + step, :], writes=[B_dst])
        banks = []
        for i in range(6):
            banks.append((es0.enter_context(nc.psum_tensor(f"bank{i}", [128, 512], F32)), Buf(f"bank{i}")))
        bankrr = [0]
        bankmod = [6]

        def bank():
            i = bankrr[0]
            bankrr[0] = (i + 1) % bankmod[0]
            return banks[i]
        psT2 = [es0.enter_context(nc.psum_tensor(f"psT{i}", [128, 1024], BF16)) for i in range(2)]
        B_psT = [Buf("psT0"), Buf("psT1")]

        identb, B_identb = sb(es0, "identb", [128, 128], BF16)
        identf, B_identf = sb(es0, "identf", [128, 128], F32)
        onesb, B_onesb = sb(es0, "onesb", [128, 128], BF16)
        k.dma("pool", identb[:], cn["ident"], writes=[B_identb])
        k.dma("sp", identf[:], cn["ident"], writes=[B_identf])
        k.dma("pool", onesb[:], cn["ones"], writes=[B_onesb])
        mcol, B_mcol = sb(es0, "mcol", [128, 8, 32], F32)

        def rms_rstd(E_sq, src_ap, src_bufs, width, ss, B_ss, junk, B_junk, rstd, B_rstd):
            k.op(act, lambda e: e.activation(out=junk, in_=src_ap, func=AF.Square, accum_out=ss), reads=src_bufs, writes=[B_junk, B_ss])
            k.op(dve, lambda e: e.tensor_scalar(out=rstd, in0=ss, scalar1=1.0 / width, scalar2=EPS, op0=ALU.mult, op1=ALU.add), reads=[B_ss], writes=[B_rstd])
            k.op(act, lambda e: e.activation(out=rstd, in_=rstd, func=AF.Sqrt), reads=[B_rstd], writes=[B_rstd])
            k.op(dve, lambda e: e.reciprocal(out=rstd, in_=rstd), reads=[B_rstd], writes=[B_rstd])

        with ExitStack() as es:
            cs, B_cs = sb(es, "cs", [128, 2, 32], F32)
            wa = [sb(es, f"wa{i}", [128, 32, 512], F32) for i in range(2)]
            brow, B_brow = sb(es, "brow", [2, 512], F32)
            mrow, B_mrow = sb(es, "mrow", [2, 512], F32)
            for r in range(2):
                k.dma("sp", cs[:, r, :], cvec[r:r + 1, :].rearrange("o (p k) -> p (o k)", k=32), writes=[B_cs])
            k.op(act, lambda e: e.activation(out=cs[:], in_=cs[:], func=AF.Silu), reads=[B_cs], writes=[B_cs])
            wav = w_ada.rearrange("(p k) n -> p k n", k=32)
            for ch in range(48):
                wt, B_wt = wa[ch % 2]
                k.dma("sp", wt[:], wav[:, :, ch * 512:(ch + 1) * 512], writes=[B_wt])
                for r in range(2):
                    k.dma("sp", brow[r:r + 1, :], b_ada[0:1, ch * 512:(ch + 1) * 512], writes=[B_brow])
                ps, B_ps = bank()
                for kk in range(32):
                    k.op(pe, lambda e: e.matmul(ps[0:2, :], lhsT=cs[:, :, kk], rhs=wt[:, kk, :], start=(kk == 0), stop=(kk == 31)),
                         reads=[B_cs, B_wt], writes=[B_ps], inc=(kk == 31))
                k.op(dve, lambda e: e.tensor_tensor(out=mrow[:], in0=ps[0:2, :], in1=brow[:], op=ALU.add), reads=[B_ps, B_brow], writes=[B_mrow])
                k.dma("sp", mod_d[:, ch * 512:(ch + 1) * 512], mrow[:], reads=[B_mrow], writes=[B_mod])
            va, B_va = sb(es, "va", [128, 128], F32)
            vb, B_vb = sb(es, "vb", [128, 128], F32)
            srcA = [mod_d[0:1, 0:D], mod_d[0:1, D:2 * D], mod_d[1:2, 0:D], mod_d[1:2, D:2 * D]]
            srcB = [mod_d[0:1, 3 * D:4 * D], mod_d[0:1, 4 * D:5 * D], norm_g[0:1, :], norm_g[1:2, :]]
            for i in range(4):
                k.dma("sp", va[32 * i:32 * i + 32, :], srcA[i].rearrange("o (k p) -> (o k) p", p=128), reads=[B_mod], writes=[B_va])
                k.dma("sp", vb[32 * i:32 * i + 32, :], srcB[i].rearrange("o (k p) -> (o k) p", p=128), reads=[B_mod], writes=[B_vb])
            psa, B_psa = bank()
            k.op(pe, lambda e: e.transpose(psa[:, 0:128], va[:], identf[:]), reads=[B_va, B_identf], writes=[B_psa])
            k.op(pe, lambda e: e.transpose(psa[:, 128:256], vb[:], identf[:]), reads=[B_vb, B_identf], writes=[B_psa])
            k.op(dve, lambda e: e.tensor_copy(out=mcol[:].rearrange("p a k -> p (a k)"), in_=psa[:, 0:256]), reads=[B_psa], writes=[B_mcol])
            for (isc, ig) in [(1, 6), (3, 6), (5, 7)]:
                k.op(dve, lambda e: e.scalar_tensor_tensor(out=mcol[:, isc, :], in0=mcol[:, isc, :], scalar=1.0, in1=mcol[:, ig, :], op0=ALU.add, op1=ALU.mult),
                     reads=[B_mcol], writes=[B_mcol])
        k.barrier()
        if stop_after == "A":
            o = nc.dram_tensor("dbg_mod", [2, 6 * D], F32, kind="ExternalOutput").ap()
            t, B_t = sb(es0, "dbgt", [2, 6 * D], F32)
            k.dma("sp", t[:], mod_d, reads=[B_mod], writes=[B_t])
            k.dma("sp", o, t[:], reads=[B_t], is_output=True)
            o2 = nc.dram_tensor("dbg_mcol", [128, 256], F32, kind="ExternalOutput").ap()
            k.dma("sp", o2, mcol[:].rearrange("p a k -> p (a k)"), reads=[B_mcol], is_output=True)
            k.finish()
            return nc

        def make_hT(es_tmp, xt, B_xt, hT_ap, B_hT_, ish, iG, tmp):
            xs, B_xs, ss, B_ss, rstd, B_rstd = tmp
            k.op(act, lambda e: e.activation(out=xs[:], in_=xt, func=AF.Square, accum_out=ss[:]), reads=[B_xt], writes=[B_xs, B_ss])
            k.op(dve, lambda e: e.tensor_scalar(out=rstd[:], in0=ss[:], scalar1=1.0 / D, scalar2=EPS, op0=ALU.mult, op1=ALU.add), reads=[B_ss], writes=[B_rstd])
            k.op(act, lambda e: e.activation(out=rstd[:], in_=rstd[:], func=AF.Sqrt), reads=[B_rstd], writes=[B_rstd])
            k.op(dve, lambda e: e.reciprocal(out=rstd[:], in_=rstd[:]), reads=[B_rstd], writes=[B_rstd])
            k.op(dve, lambda e: e.tensor_scalar(out=xs[:], in0=xt, scalar1=rstd[:, 0:1], scalar2=None, op0=ALU.mult), reads=[B_xt, B_rstd], writes=[B_xs])
            for q4 in range(8):
                hb = q4 % 2
                for i in range(4):
                    kt = q4 * 4 + i
                    k.op(pe, lambda e: e.transpose(psT2[hb][:, i * 128:(i + 1) * 128], xs[:, kt * 128:(kt + 1) * 128], identb[:]),
                         reads=[B_xs, B_identb], writes=[B_psT[hb]], inc=(i == 3))
                for i in range(4):
                    kt = q4 * 4 + i
                    k.op(act, lambda e: e.activation(out=hT_ap[:, kt, :], in_=psT2[hb][:, i * 128:(i + 1) * 128], func=AF.Identity,
                                                     scale=mcol[:, iG, kt:kt + 1], bias=mcol[:, ish, kt:kt + 1]),
                         reads=[B_psT[hb], B_mcol], writes=[B_hT_])

        with ExitStack() as es_hy:
            hid2T, B_hid2 = sb(es_hy, "hid2T", [64, L], F32)
            fcol, B_fcol = sb(es_hy, "fcol", [64, 8], F32)
            with ExitStack() as es:
                featsT, B_ft = sb(es, "featsT", [33, L], F32)
                h1T, B_h1 = sb(es, "h1T", [64, L], F32)
                fw1, B_fw1 = sb(es, "fw1", [33, 64], F32)
                fw2, B_fw2 = sb(es, "fw2", [64, 64], F32)
                pre, B_pre = sb(es, "pre", [64, 512], F32)
                rr, B_rr = sb(es, "rr", [64, 512], F32)
                k.dma("sp", featsT[:], cn["featsT"], writes=[B_ft])
                k.dma("sp", fw1[:], f_w1, writes=[B_fw1])
                k.dma("sp", fw2[:], f_w2, writes=[B_fw2])
                k.dma("sp", fcol[:, 0:1], f_b1, writes=[B_fcol])
                k.dma("sp", fcol[:, 1:2], f_b2, writes=[B_fcol])
                k.dma("sp", fcol[:, 2:3], f_freq, writes=[B_fcol])
                k.op(dve, lambda e: e.tensor_tensor(out=fcol[:, 3:4], in0=fcol[:, 0:1], in1=fcol[:, 2:3], op=ALU.mult), reads=[B_fcol], writes=[B_fcol])
                k.op(dve, lambda e: e.tensor_tensor(out=fcol[:, 4:5], in0=fcol[:, 1:2], in1=fcol[:, 2:3], op=ALU.mult), reads=[B_fcol], writes=[B_fcol])
                for layer in range(2):
                    for chn in range(16):
                        cs_ = slice(chn * 512, (chn + 1) * 512)
                        ps, B_ps = bank()
                        if layer == 0:
                            k.op(pe, lambda e: e.matmul(ps[0:64, :], lhsT=fw1[0:33, :], rhs=featsT[0:33, cs_], start=True, stop=True), reads=[B_fw1, B_ft], writes=[B_ps])
                        else:
                            k.op(pe, lambda e: e.matmul(ps[0:64, :], lhsT=fw2[0:64, :], rhs=h1T[0:64, cs_], start=True, stop=True), reads=[B_fw2, B_h1], writes=[B_ps])
                        k.op(dve, lambda e: e.tensor_scalar(out=pre[:], in0=ps[0:64, :], scalar1=fcol[:, 2:3], scalar2=fcol[:, 3 + layer:4 + layer], op0=ALU.mult, op1=ALU.add),
                             reads=[B_ps, B_fcol], writes=[B_pre])
                        k.op(dve, lambda e: e.tensor_scalar(out=rr[:], in0=pre[:], scalar1=1.0 / (2 * PI), scalar2=MAGIC, op0=ALU.mult, op1=ALU.add), reads=[B_pre], writes=[B_rr])
                        k.op(dve, lambda e: e.tensor_scalar(out=rr[:], in0=rr[:], scalar1=-MAGIC, scalar2=-2 * PI, op0=ALU.add, op1=ALU.mult), reads=[B_rr], writes=[B_rr])
                        k.op(dve, lambda e: e.tensor_tensor(out=rr[:], in0=rr[:], in1=pre[:], op=ALU.add), reads=[B_rr, B_pre], writes=[B_rr])
                        dst, B_dst = (h1T, B_h1) if layer == 0 else (hid2T, B_hid2)
                        k.op(act, lambda e: e.activation(out=dst[0:64, cs_], in_=rr[:], func=AF.Sin), reads=[B_rr], writes=[B_dst])
            k.barrier()
            if stop_after == "A2":
                o = nc.dram_tensor("dbg_hid2T", [64, L], F32, kind="ExternalOutput").ap()
                k.dma("sp", o, hid2T[:], reads=[B_hid2], is_output=True)
                k.finish()
                return nc

            f256, B_f256 = sb(es_hy, "f256", [128, 512], F32)
            twc, B_twc = sb(es_hy, "twc", [128, 512], F32)
            tws, B_tws = sb(es_hy, "tws", [128, 512], F32)
            c64, B_c64 = sb(es_hy, "c64", [128, 128], F32)
            s64, B_s64 = sb(es_hy, "s64", [128, 128], F32)
            ns64, B_ns64 = sb(es_hy, "ns64", [128, 128], F32)
            icis, B_icis = sb(es_hy, "icis", [128, 512], F32)
            negt, B_negt = sb(es_hy, "negt", [128, 64], F32)
            selb, B_selb = sb(es_hy, "selb", [128, 32], BF16)
            for hh_ in range(2):
                k.dma("sp", twc[:, hh_ * 256:(hh_ + 1) * 256], cn["twc"], writes=[B_twc])
                k.dma("sp", tws[:, hh_ * 256:(hh_ + 1) * 256], cn["tws"], writes=[B_tws])
            for t_, n_, b_ in [(f256, "f256", B_f256), (c64, "c64", B_c64), (s64, "s64", B_s64),
                               (ns64, "ns64", B_ns64), (icis, "icis", B_icis), (negt, "negt", B_negt)]:
                k.dma("sp", t_[:], cn[n_], writes=[b_])
            k.dma("pool", selb[:], cn["sel"], writes=[B_selb])
            B_fc = [B_f256, B_twc, B_tws, B_c64, B_s64, B_ns64, B_icis]
            P_ext, B_P = sb(es_hy, "P_ext", [128, 66, 192], BF16)
            V, B_V = sb(es_hy, "V", [128, 64, 64], F32)
            X1, B_X1 = sb(es_hy, "X1", [128, 64, 64], BF16)
            X2, B_X2 = sb(es_hy, "X2", [128, 64, 64], BF16)
            Yout, B_Y = sb(es_hy, "Yout", [128, 64, 64], BF16)
            wg, B_wg = sb(es_hy, "wg", [128, 32, 192], BF16)
            hTb = [sb(es_hy, f"hTb{i}", [128, 32, 128], BF16) for i in range(2)]
            xt, B_xt = sb(es_hy, "xt", [128, D], F32)
            ss, B_ss = sb(es_hy, "ss", [128, 1], F32)
            rstd, B_rstd = sb(es_hy, "rstd", [128, 1], F32)
            cw, B_cw = sb(es_hy, "cw", [128, 3, 192], F32)
            cb, B_cb = sb(es_hy, "cb", [128, 192], F32)
            Hf, B_Hf = sb(es_hy, "Hf", [128, 2, 32, 64], F32)
            dec = es_hy.enter_context(nc.sbuf_tensor("dec", [128, 2, 8, 32], F32))
            B_dec = [Buf("dec0"), Buf("dec1")]
            w3g, B_w3g = sb(es_hy, "w3g", [64, 256], F32)
            drow, B_drow = sb(es_hy, "drow", [128, 64], F32)
            brow, B_brow = sb(es_hy, "browh", [1, 128], F32)
            NSET = 3
            FS = []
            for si_ in range(NSET):
                d_ = {}
                for nm_ in ("Bsb", "Kf", "Ysb"):
                    d_[nm_] = sb(es_hy, f"{nm_}{si_}", [128, 512], F32)
                d_["Esb"] = d_["Bsb"]
                d_["ETsb"] = d_["Ysb"]
                T12_ = es_hy.enter_context(nc.sbuf_tensor(f"tt12_{si_}", [128, 512], F32))
                T34_ = es_hy.enter_context(nc.sbuf_tensor(f"tt34_{si_}", [128, 512], F32))
                d_["tt"] = [(T12_[:, 0:256], Buf()), (T12_[:, 256:512], Buf()), (T34_[:, 0:256], Buf()), (T34_[:, 256:512], Buf())]
                d_["ttw"] = (T12_, T34_)
                d_["rr"] = [0]
                FS.append(d_)
            B_Vp = [Buf(f"V{i}") for i in range(32)]
            B_Yp = [Buf(f"Y{i}") for i in range(32)]
            xs = Yout[:].rearrange("p c n -> p (c n)")
            B_xs = B_Y
            tmpA = xt[:].rearrange("p (n c) -> p n c", c=64)
            tmpB = Hf[:].rearrange("p a c n -> p (a c n)").rearrange("p (n c) -> p n c", c=64)
            xv = x.rearrange("(a b) d -> b a d", b=64)

            def cmul(src, B_src, ca, cb_, B_cab, dst, B_dst_, conj, tt):
                sr, si = src[:, 0:256], src[:, 256:512]
                (t1, B1), (t2, B2), (t3, B3), (t4, B4) = tt
                k.op(dve, lambda e: e.tensor_tensor(out=t1, in0=sr, in1=ca, op=ALU.mult), reads=[B_src] + B_cab, writes=[B1])
                k.op(dve, lambda e: e.tensor_tensor(out=t2, in0=si, in1=cb_, op=ALU.mult), reads=[B_src] + B_cab, writes=[B2])
                k.op(dve, lambda e: e.tensor_tensor(out=t3, in0=si, in1=ca, op=ALU.mult), reads=[B_src] + B_cab, writes=[B3])
                k.op(dve, lambda e: e.tensor_tensor(out=t4, in0=sr, in1=cb_, op=ALU.mult), reads=[B_src] + B_cab, writes=[B4])
                k.op(pool, lambda e: e.tensor_tensor(out=dst[:, 0:256], in0=t1, in1=t2, op=(ALU.add if conj else ALU.subtract)), reads=[B1, B2], writes=[B_dst_])
                k.op(pool, lambda e: e.tensor_tensor(out=dst[:, 256:512], in0=t3, in1=t4, op=(ALU.subtract if conj else ALU.add)), reads=[B3, B4], writes=[B_dst_])

            def cmul_tw(src, B_src, dst, B_dst_, conj, fs):
                T12_, T34_ = fs["ttw"]
                (_, B1), (_, B2), (_, B3), (_, B4) = fs["tt"]
                k.op(dve, lambda e: e.tensor_tensor(out=T12_[:], in0=src[:, 0:512], in1=twc[:], op=ALU.mult), reads=[B_src, B_twc], writes=[B1, B2])
                k.op(dve, lambda e: e.tensor_tensor(out=T34_[:], in0=src[:, 0:512], in1=tws[:], op=ALU.mult), reads=[B_src, B_tws], writes=[B3, B4])
                k.op(pool, lambda e: e.tensor_tensor(out=dst[:, 0:256], in0=T12_[:, 0:256], in1=T34_[:, 256:512], op=(ALU.add if conj else ALU.subtract)),
                     reads=[B1, B4], writes=[B_dst_])
                k.op(pool, lambda e: e.tensor_tensor(out=dst[:, 256:512], in0=T12_[:, 256:512], in1=T34_[:, 0:256], op=(ALU.subtract if conj else ALU.add)),
                     reads=[B2, B3], writes=[B_dst_])

            def dft64(src, B_src, ps, B_ps, inverse, parts="both"):
                sA, sB = (ns64, s64) if inverse else (s64, ns64)
                if parts in ("both", "re"):
                    k.op(pe, lambda e: e.matmul(ps[:, 0:256], lhsT=c64[:], rhs=src[:, 0:256], start=True, stop=False), reads=[B_src, B_c64], writes=[B_ps], inc=False)
                    k.op(pe, lambda e: e.matmul(ps[:, 0:256], lhsT=sA[:], rhs=src[:, 256:512], start=False, stop=True), reads=[B_src, B_s64, B_ns64], writes=[B_ps], inc=(parts == "re"))
                if parts in ("both", "im"):
                    k.op(pe, lambda e: e.matmul(ps[:, 256:512], lhsT=c64[:], rhs=src[:, 256:512], start=True, stop=False), reads=[B_src, B_c64], writes=[B_ps], inc=False)
                    k.op(pe, lambda e: e.matmul(ps[:, 256:512], lhsT=sB[:], rhs=src[:, 0:256], start=False, stop=True), reads=[B_src, B_s64, B_ns64], writes=[B_ps])

            def sbank(fs, si_, which):
                return banks[2 * si_ + which]

            def pair_pipeline(si_, o_, p_, cp):
                fs = FS[si_]
                (Bsb, B_Bsb), (Kf, B_Kf), (Ysb, B_Ysb), (Esb, B_Esb), (ETsb, B_ETsb) = fs["Bsb"], fs["Kf"], fs["Ysb"], fs["Esb"], fs["ETsb"]
                tt = fs["tt"]
                ip = cp // 2
                psK, B_psK = sbank(fs, si_, 0)
                for d_ in range(2):
                    psA, B_psA = sbank(fs, si_, 1)
                    k.op(pe, lambda e: e.matmul(psA[:], lhsT=Hf[:, d_, 2 * p_:2 * p_ + 2, :].rearrange("p c n -> p (c n)"), rhs=f256[:], start=True, stop=True),
                         reads=[B_Hf, B_f256], writes=[B_psA])
                    yield
                    cmul_tw(psA, B_psA, Bsb, B_Bsb, True, fs)
                    yield
                    dft64(Bsb, B_Bsb, psK, B_psK, inverse=False, parts=("re" if d_ == 0 else "im"))
                    yield
                k.op(act, lambda e: e.copy(out=Kf[:], in_=psK[:]), reads=[B_psK], writes=[B_Kf])
                psA, B_psA = sbank(fs, si_, 0)
                k.op(pe, lambda e: e.matmul(psA[:], lhsT=V[:, cp:cp + 2, :].rearrange("p c n -> p (c n)"), rhs=f256[:], start=True, stop=True),
                     reads=[B_Vp[ip], B_f256], writes=[B_psA])
                yield
                cmul_tw(psA, B_psA, Bsb, B_Bsb, True, fs)
                yield
                psX, B_psX = sbank(fs, si_, 1)
                dft64(Bsb, B_Bsb, psX, B_psX, inverse=False)
                yield
                cmul(psX, B_psX, Kf[:, 0:256], Kf[:, 256:512], [B_Kf], Ysb, B_Ysb, False, tt)
                yield
                psD, B_psD = sbank(fs, si_, 0)
                dft64(Ysb, B_Ysb, psD, B_psD, inverse=True)
                yield
                cmul_tw(psD, B_psD, Esb, B_Esb, False, fs)
                yield
                psT, B_psT_ = sbank(fs, si_, 1)
                for h_ in range(2):
                    for ri in range(2):
                        bi = h_ * 2 + ri
                        k.op(pe, lambda e: e.transpose(psT[:, bi * 128:(bi + 1) * 128], Esb[:, ri * 256 + h_ * 128: ri * 256 + (h_ + 1) * 128], identf[:]),
                             reads=[B_Esb, B_identf], writes=[B_psT_], inc=(bi == 3))
                yield
                k.op(act, lambda e: e.copy(out=ETsb[:], in_=psT[:]), reads=[B_psT_], writes=[B_ETsb])
                yield
                psY, B_psY = sbank(fs, si_, 0)
                for bi in range(4):
                    k.op(pe, lambda e: e.matmul(psY[:, 0:128], lhsT=icis[:, bi * 128:(bi + 1) * 128], rhs=ETsb[:, bi * 128:(bi + 1) * 128], start=(bi == 0), stop=(bi == 3)),
                         reads=[B_icis, B_ETsb], writes=[B_psY], inc=(bi == 3))
                yield
                if o_ == 0:
                    k.op(dve, lambda e: e.tensor_tensor(out=V[:, cp:cp + 2, :].rearrange("p c n -> p (c n)"), in0=psY[:, 0:128],
                                                        in1=X1[:, cp:cp + 2, :].rearrange("p c n -> p (c n)"), op=ALU.mult), reads=[B_psY, B_X1], writes=[B_Vp[ip]])
                else:
                    k.op(dve, lambda e: e.tensor_tensor(out=Yout[:, cp:cp + 2, :].rearrange("p c n -> p (c n)"), in0=psY[:, 0:128],
                                                        in1=X2[:, cp:cp + 2, :].rearrange("p c n -> p (c n)"), op=ALU.mult), reads=[B_psY, B_X2], writes=[B_Yp[ip]])

            for g in range(NG):
                ch0 = g * 64
                cast_load(wg, w_hy[:, g * 192:(g + 1) * 192].rearrange("(kt p) n -> p kt n", p=128), B_wg)
                for n2 in range(64):
                    hTt, B_h = hTb[n2 % 2]
                    if g == 0:
                        k.dma("sp", xt[:], xv[n2], writes=[B_xt])
                        make_hT(None, xt[:], B_xt, hTt, B_h, 0, 1, (Yout[:].rearrange("p c n -> p (c n)"), B_Y, ss, B_ss, rstd, B_rstd))
                        k.dma("sp", hT_d[n2], hTt[:].rearrange("p k t -> p (k t)"), reads=[B_h], writes=[B_hT[n2]])
                    else:
                        k.dma("sp", hTt[:].rearrange("p k t -> p (k t)"), hT_d[n2], reads=[B_hT[n2]], writes=[B_h])
                    ps, B_ps = bank()
                    for kt in range(32):
                        k.op(pe, lambda e: e.matmul(ps[:, 0:192], lhsT=hTt[:, kt, :], rhs=wg[:, kt, :], start=(kt == 0), stop=(kt == 31)),
                             reads=[B_h, B_wg], writes=[B_ps], inc=(kt == 31))
                    k.op(act, lambda e: e.copy(out=P_ext[:, n2 + 1, :], in_=ps[:, 0:192]), reads=[B_ps], writes=[B_P])
                if stop_after == "C1" and g == 0:
                    o = nc.dram_tensor("dbg_P", [128, 66 * 192], BF16, kind="ExternalOutput").ap()
                    k.dma("sp", o, P_ext[:].rearrange("p a c -> p (a c)"), reads=[B_P], is_output=True)
                    o2 = nc.dram_tensor("dbg_hT", [2, 128, D], BF16, kind="ExternalOutput").ap()
                    for i_ in range(2):
                        k.dma("sp", hTb[i_][0][:].rearrange("p k t -> p (k t)"), hT_d[i_], reads=[B_hT[i_]], writes=[hTb[i_][1]])
                        k.dma("sp", o2[i_], hTb[i_][0][:].rearrange("p k t -> p (k t)"), reads=[hTb[i_][1]], is_output=True)
                    o3 = nc.dram_tensor("dbg_wg", [128, 32 * 192], BF16, kind="ExternalOutput").ap()
                    k.dma("sp", o3, wg[:].rearrange("p a c -> p (a c)"), reads=[B_wg], is_output=True)
                    k.finish()
                    return nc
                k.dma("sp", P_ext[1:128, 0, :], P_ext[0:127, 64, :], reads=[B_P], writes=[B_P])
                k.dma("pool", P_ext[0:1, 0, :], cn["zeros"][0:1, 0:192], writes=[B_P])
                k.dma("sp", P_ext[0:127, 65, :], P_ext[1:128, 1, :], reads=[B_P], writes=[B_P])
                k.dma("pool", P_ext[127:128, 65, :], cn["zeros"][0:1, 0:192], writes=[B_P])
                for kk in range(3):
                    k.dma("sp", cw[:, kk, :], hy_cw[kk:kk + 1, g * 192:(g + 1) * 192].partition_broadcast(128), writes=[B_cw])
                k.dma("sp", cb[:], hy_cb[0:1, g * 192:(g + 1) * 192].partition_broadcast(128), writes=[B_cb])
                for part, (dst, B_dst) in enumerate([(V, B_Vp), (X1, [B_X1]), (X2, [B_X2])]):
                    cs_ = slice(part * 64, part * 64 + 64)
                    bc = lambda kk: cw[:, kk, cs_].unsqueeze(1).broadcast_to([128, 64, 64])
                    k.op(dve, lambda e: e.tensor_tensor(out=tmpA, in0=P_ext[:, 0:64, cs_], in1=bc(0), op=ALU.mult), reads=[B_P, B_cw], writes=[B_xt])
                    k.op(pool, lambda e: e.tensor_tensor(out=tmpB, in0=P_ext[:, 1:65, cs_], in1=bc(1), op=ALU.mult), reads=[B_P, B_cw], writes=[B_Hf])
                    k.op(dve, lambda e: e.tensor_tensor(out=tmpA, in0=tmpA, in1=tmpB, op=ALU.add), reads=[B_xt, B_Hf], writes=[B_xt])
                    k.op(pool, lambda e: e.tensor_tensor(out=tmpB, in0=P_ext[:, 2:66, cs_], in1=bc(2), op=ALU.mult), reads=[B_P, B_cw], writes=[B_Hf])
                    k.op(dve, lambda e: e.tensor_tensor(out=tmpA, in0=tmpA, in1=tmpB, op=ALU.add), reads=[B_xt, B_Hf], writes=[B_xt])
                    k.op(dve, lambda e: e.tensor_tensor(out=dst[:].rearrange("p c n -> p n c"), in0=tmpA, in1=cb[:, cs_].unsqueeze(1).broadcast_to([128, 64, 64]), op=ALU.add),
                         reads=[B_xt, B_cb], writes=B_dst)
                if stop_after == "C2" and g == 0:
                    for nm, t_, b_, shp in [("dbg_V", V, B_Vp, [128, 4096])]:
                        o = nc.dram_tensor(nm, shp, F32, kind="ExternalOutput").ap()
                        k.dma("sp", o, t_[:].rearrange("p c n -> p (c n)"), reads=b_, is_output=True)
                    k.finish()
                    return nc
                k.dma("sp", w3g[:], f_w3[:, g * 256:(g + 1) * 256], writes=[B_w3g])
                k.dma("sp", drow[:], cn["deltas"][0:1, g * 64:(g + 1) * 64].partition_broadcast(128), writes=[B_drow])
                k.dma("sp", brow[:], hy_bias[0:1, g * 128:(g + 1) * 128], writes=[B_brow])
                for o_ in range(2):
                    for sbi in range(2):
                        c0 = sbi * 32
                        wc0 = (o_ * 2 + sbi) * 64
                        for blk in range(8):
                            db = blk % 2
                            for l_ in range(8):
                                n2 = blk * 8 + l_
                                k.op(act, lambda e: e.activation(out=dec[:, db, l_, :], in_=drow[:, c0:c0 + 32], func=AF.Exp, scale=negt[:, n2:n2 + 1]),
                                     reads=[B_drow, B_negt], writes=[B_dec[db]])
                            ps, B_ps = bank()
                            for l_ in range(8):
                                n2 = blk * 8 + l_
                                k.op(pe, lambda e: e.matmul(ps[:, l_ * 64:(l_ + 1) * 64], lhsT=hid2T[0:64, n2:L:64], rhs=w3g[0:64, wc0:wc0 + 64], start=True, stop=True),
                                     reads=[B_hid2, B_w3g], writes=[B_ps], inc=(l_ == 7))
                            for d_ in range(2):
                                k.op(dve, lambda e: e.tensor_tensor(out=Hf[:, d_, :, blk * 8:(blk + 1) * 8].rearrange("p c n -> p n c"),
                                                                    in0=ps[:, 0:512].rearrange("p (n d c) -> p n d c", d=2, c=32)[:, :, d_, :], in1=dec[:, db, :, :], op=ALU.mult),
                                     reads=[B_ps, B_dec[db]], writes=[B_Hf])
                        k.op(dve, lambda e: e.tensor_tensor(out=Hf[0:1, 0, :, 0], in0=Hf[0:1, 0, :, 0], in1=brow[0:1, o_ * 64 + c0: o_ * 64 + c0 + 32], op=ALU.add),
                             reads=[B_Hf, B_brow], writes=[B_Hf])
                        k.op(dve, lambda e: e.memset(Hf[0:1, 1, :, 0], 0.0), writes=[B_Hf])
                        Hf0 = Hf[:, 0, :, :].rearrange("p c n -> p (c n)")
                        Hf1 = Hf[:, 1, :, :].rearrange("p c n -> p (c n)")
                        k.op(pool, lambda e: e.tensor_tensor(out=Hf1, in0=Hf0, in1=Hf1, op=ALU.subtract), reads=[B_Hf], writes=[B_Hf])
                        k.op(pool, lambda e: e.tensor_tensor(out=Hf0, in0=Hf0, in1=Hf0, op=ALU.add), reads=[B_Hf], writes=[B_Hf])
                        k.op(pool, lambda e: e.tensor_tensor(out=Hf0, in0=Hf0, in1=Hf1, op=ALU.subtract), reads=[B_Hf], writes=[B_Hf])
                        todo = [(p_, c0 + 2 * p_) for p_ in range(16)]
                        active = []
                        for si_ in range(NSET):
                            p_, cp = todo.pop(0)
                            active.append([si_, pair_pipeline(si_, o_, p_, cp)])
                        while active:
                            for ent in list(active):
                                try:
                                    next(ent[1])
                                except StopIteration:
                                    if todo:
                                        p_, cp = todo.pop(0)
                                        ent[1] = pair_pipeline(ent[0], o_, p_, cp)
                                    else:
                                        active.remove(ent)
                ysel = P_ext[:].rearrange("p a c -> p (a c)")
                Yf = Yout[:].rearrange("p c n -> p (c n)")
                for cch in range(8):
                    ps, B_ps = bank()
                    k.op(pe, lambda e: e.matmul(ps[0:32, :], lhsT=selb[:, 0:32], rhs=Yf[:, cch * 512:(cch + 1) * 512], start=True, stop=True), reads=[B_selb] + B_Yp, writes=[B_ps])
                    k.op(act, lambda e: e.copy(out=ysel[0:32, cch * 512:(cch + 1) * 512], in_=ps[0:32, :]), reads=[B_ps], writes=[B_P])
                k.dma("sp", yT_d[ch0:ch0 + 64, :].rearrange("c (i n) -> i c n", n=64), ysel[0:32, 0:4096].rearrange("i (c n) -> i c n", n=64), reads=[B_P], writes=[B_yT])
                if stop_after == "C3" and g == 0:
                    o = nc.dram_tensor("dbg_Y", [128, 4096], F32, kind="ExternalOutput").ap()
                    k.op(dve, lambda e: e.tensor_copy(out=xt[:], in_=Yf), reads=B_Yp, writes=[B_xt])
                    k.dma("sp", o, xt[:], reads=[B_xt], is_output=True)
                    o2 = nc.dram_tensor("dbg_Z", [128, 4096], F32, kind="ExternalOutput").ap()
                    k.dma("sp", o2, V[:].rearrange("p c n -> p (c n)"), reads=B_Vp, is_output=True)
                    k.finish()
                    return nc

        k.barrier()
        NKT = 66
        NK = NKT * 128
        with ExitStack() as es_att:
            qnT, B_qnT = sb(es_att, "qnT", [128, 8, 2048], BF16)
            ss, B_ss = sb(es_att, "ss_a", [128, 2], F32)
            rstd, B_rstd = sb(es_att, "rstd_a", [128, 1], F32)
            with ExitStack() as es:
                wq, B_wq = sb(es, "wq", [128, 32, QL], BF16)
                xt, B_xt = sb(es, "xt_q", [128, D], F32)
                xs, B_xs = sb(es, "xs_q", [128, D], BF16)
                hT1, B_hT1 = sb(es, "hT1", [128, 32, 128], BF16)
                gqa, B_gqa = sb(es, "gqa", [128, QL], F32)
                qn, B_qn = sb(es, "qn", [128, QL], BF16)
                ss1, B_ss1 = sb(es, "ss1", [128, 1], F32)
                cast_load(wq, w_q.rearrange("(kt p) n -> p kt n", p=128), B_wq)
                k.dma("sp", gqa[:], g_qa.partition_broadcast(128), writes=[B_gqa])
                for m in range(16):
                    k.dma("sp", xt[:], x_own[m * 128:(m + 1) * 128, :], writes=[B_xt])
                    make_hT(None, xt[:], B_xt, hT1, B_hT1, 0, 1, (xs, B_xs, ss1, B_ss1, rstd, B_rstd))
                    pq = [bank(), bank()]
                    for hf in range(2):
                        ps, B_ps = pq[hf]
                        for kt in range(32):
                            k.op(pe, lambda e: e.matmul(ps[:], lhsT=hT1[:, kt, :], rhs=wq[:, kt, hf * 512:(hf + 1) * 512], start=(kt == 0), stop=(kt == 31)),
                                 reads=[B_hT1, B_wq], writes=[B_ps], inc=(kt == 31))
                        k.op(act, lambda e: e.activation(out=qn[:, hf * 512:(hf + 1) * 512], in_=ps[:], func=AF.Square, accum_out=ss[:, hf:hf + 1]), reads=[B_ps], writes=[B_qn, B_ss])
                    k.op(dve, lambda e: e.tensor_tensor(out=rstd[:], in0=ss[:, 0:1], in1=ss[:, 1:2], op=ALU.add), reads=[B_ss], writes=[B_rstd])
                    k.op(dve, lambda e: e.tensor_scalar(out=rstd[:], in0=rstd[:], scalar1=1.0 / QL, scalar2=EPS, op0=ALU.mult, op1=ALU.add), reads=[B_rstd], writes=[B_rstd])
                    k.op(act, lambda e: e.activation(out=rstd[:], in_=rstd[:], func=AF.Sqrt), reads=[B_rstd], writes=[B_rstd])
                    k.op(dve, lambda e: e.reciprocal(out=rstd[:], in_=rstd[:]), reads=[B_rstd], writes=[B_rstd])
                    for hf in range(2):
                        ps, B_ps = pq[hf]
                        k.op(dve, lambda e: e.scalar_tensor_tensor(out=qn[:, hf * 512:(hf + 1) * 512], in0=ps[:], scalar=rstd[:, 0:1], in1=gqa[:, hf * 512:(hf + 1) * 512],
                                                                   op0=ALU.mult, op1=ALU.mult), reads=[B_ps, B_rstd, B_gqa], writes=[B_qn])
                    for q4 in range(2):
                        for i in range(4):
                            kt = q4 * 4 + i
                            k.op(pe, lambda e: e.transpose(psT2[q4][:, i * 128:(i + 1) * 128], qn[:, kt * 128:(kt + 1) * 128], identb[:]),
                                 reads=[B_qn, B_identb], writes=[B_psT[q4]], inc=(i == 3))
                        k.op(act, lambda e: e.copy(out=qnT[:, q4 * 4:(q4 + 1) * 4, m * 128:(m + 1) * 128], in_=psT2[q4][:, 0:512].rearrange("p (k t) -> p k t", t=128)),
                             reads=[B_psT[q4]], writes=[B_qnT])
            k.barrier()
            kvnT, B_kvnT = sb(es_att, "kvnT", [128, 4, NK], BF16)
            krT, B_krT = sb(es_att, "krT", [64, NK], BF16)
            ssr, B_ssr = sb(es_att, "ssr", [128, NKT], F32)
            with ExitStack() as es:
                wkv, B_wkv = sb(es, "wkv", [128, 32, 576], BF16)
                hTk = [sb(es, f"hTk{i}", [128, 32, 128], BF16) for i in range(2)]
                xt, B_xt = sb(es, "xt_k", [128, D], F32)
                xs, B_xs = sb(es, "xs_k", [128, D], BF16)
                gkva, B_gkva = sb(es, "gkva", [128, KVL], F32)
                gkr, B_gkr = sb(es, "gkr", [128, 64], F32)
                ropek, B_ropek = sb(es, "ropek", [128, 2, 64], F32)
                kvn, B_kvn = sb(es, "kvn", [128, KVL], BF16)
                krg, B_krg = sb(es, "krg", [128, 64], F32)
                krb, B_krb = sb(es, "krb", [128, 64], BF16)
                ra, B_ra = sb(es, "ra", [128, 2, 16], F32)
                rb, B_rb = sb(es, "rb", [128, 2, 16], F32)
                ss1, B_ss1 = sb(es, "ss1k", [128, 1], F32)
                cast_load(wkv, w_kv.rearrange("(kt p) n -> p kt n", p=128), B_wkv)
                k.dma("sp", gkva[:], g_kva.partition_broadcast(128), writes=[B_gkva])
                k.dma("sp", gkr[:], gk[0:1, 128:192].partition_broadcast(128), writes=[B_gkr])
                for idx in range(NKT):
                    hTt, B_h = hTk[idx % 2]
                    if idx < 2:
                        k.dma("sp", xt[:], ctx[idx * 128:(idx + 1) * 128, :], writes=[B_xt])
                        make_hT(None, xt[:], B_xt, hTt, B_h, 2, 3, (xs, B_xs, ss1, B_ss1, rstd, B_rstd))
                    else:
                        k.dma("sp", hTt[:].rearrange("p k t -> p (k t)"), hT_d[idx - 2], reads=[B_hT[idx - 2]], writes=[B_h])
                    psA, B_psA = bank()
                    psB, B_psB = bank()
                    for kt in range(32):
                        k.op(pe, lambda e: e.matmul(psA[:], lhsT=hTt[:, kt, :], rhs=wkv[:, kt, 0:512], start=(kt == 0), stop=(kt == 31)), reads=[B_h, B_wkv], writes=[B_psA], inc=(kt == 31))
                    for kt in range(32):
                        k.op(pe, lambda e: e.matmul(psB[:, 0:64], lhsT=hTt[:, kt, :], rhs=wkv[:, kt, 512:576], start=(kt == 0), stop=(kt == 31)), reads=[B_h, B_wkv], writes=[B_psB], inc=(kt == 31))
                    k.op(act, lambda e: e.activation(out=kvn[:], in_=psA[:], func=AF.Square, accum_out=ss[:, 0:1]), reads=[B_psA], writes=[B_kvn, B_ss])
                    k.op(dve, lambda e: e.tensor_scalar(out=rstd[:], in0=ss[:, 0:1], scalar1=1.0 / KVL, scalar2=EPS, op0=ALU.mult, op1=ALU.add), reads=[B_ss], writes=[B_rstd])
                    k.op(act, lambda e: e.activation(out=rstd[:], in_=rstd[:], func=AF.Sqrt), reads=[B_rstd], writes=[B_rstd])
                    k.op(dve, lambda e: e.reciprocal(out=rstd[:], in_=rstd[:]), reads=[B_rstd], writes=[B_rstd])
                    k.op(dve, lambda e: e.scalar_tensor_tensor(out=kvn[:], in0=psA[:], scalar=rstd[:, 0:1], in1=gkva[:], op0=ALU.mult, op1=ALU.mult), reads=[B_psA, B_rstd, B_gkva], writes=[B_kvn])
                    for i in range(4):
                        k.op(pe, lambda e: e.transpose(psT2[0][:, i * 128:(i + 1) * 128], kvn[:, i * 128:(i + 1) * 128], identb[:]), reads=[B_kvn, B_identb], writes=[B_psT[0]], inc=(i == 3))
                    k.op(act, lambda e: e.copy(out=kvnT[:, :, idx * 128:(idx + 1) * 128], in_=psT2[0][:, 0:512].rearrange("p (k t) -> p k t", t=128)), reads=[B_psT[0]], writes=[B_kvnT])
                    k.op(act, lambda e: e.activation(out=krg[:], in_=psB[:, 0:64], func=AF.Square, accum_out=ssr[:, idx:idx + 1]), reads=[B_psB], writes=[B_krg, B_ssr])
                    k.op(dve, lambda e: e.tensor_tensor(out=krg[:], in0=psB[:, 0:64], in1=gkr[:], op=ALU.mult), reads=[B_psB, B_gkr], writes=[B_krg])
                    if idx < 2:
                        k.op(act, lambda e: e.copy(out=krb[:], in_=krg[:]), reads=[B_krg], writes=[B_krb])
                    else:
                        n2 = idx % 2
                        k.dma("sp", ropek[:, n2, :], cn["ropek"][:, (idx - 2) * 64:(idx - 1) * 64], writes=[B_ropek])
                        v4 = krg[:].rearrange("p (a h f) -> p a h f", a=2, h=2)
                        o4 = krb[:].rearrange("p (a h f) -> p a h f", a=2, h=2)
                        cs_ = ropek[:, n2, 0:32].rearrange("p (a f) -> p a f", a=2)
                        sn_ = ropek[:, n2, 32:64].rearrange("p (a f) -> p a f", a=2)
                        x1_, x2_ = v4[:, :, 0, :], v4[:, :, 1, :]
                        k.op(dve, lambda e: e.tensor_tensor(out=ra[:], in0=x1_, in1=cs_, op=ALU.mult), reads=[B_krg, B_ropek], writes=[B_ra])
                        k.op(dve, lambda e: e.tensor_tensor(out=rb[:], in0=x2_, in1=sn_, op=ALU.mult), reads=[B_krg, B_ropek], writes=[B_rb])
                        k.op(dve, lambda e: e.tensor_tensor(out=o4[:, :, 0, :], in0=ra[:], in1=rb[:], op=ALU.subtract), reads=[B_ra, B_rb], writes=[B_krb])
                        k.op(dve, lambda e: e.tensor_tensor(out=ra[:], in0=x1_, in1=sn_, op=ALU.mult), reads=[B_krg, B_ropek], writes=[B_ra])
                        k.op(dve, lambda e: e.tensor_tensor(out=rb[:], in0=x2_, in1=cs_, op=ALU.mult), reads=[B_krg, B_ropek], writes=[B_rb])
                        k.op(dve, lambda e: e.tensor_tensor(out=o4[:, :, 1, :], in0=ra[:], in1=rb[:], op=ALU.add), reads=[B_ra, B_rb], writes=[B_krb])
                    k.op(pe, lambda e: e.transpose(psT2[1][0:64, 0:128], krb[:], identb[:]), reads=[B_krb, B_identb], writes=[B_psT[1]])
                    k.op(act, lambda e: e.copy(out=krT[0:64, idx * 128:(idx + 1) * 128], in_=psT2[1][0:64, 0:128]), reads=[B_psT[1]], writes=[B_krT])
            k.barrier()
            with ExitStack() as es:
                bankmod[0] = 3
                bankrr[0] = 0
                (psO, B_psO), (psL, B_psL), (psSS, B_psSS) = banks[3], banks[4], banks[5]
                wqb, B_wqb = sb(es, "wqb", [128, 8, QK], BF16)
                wkvb, B_wkvb = sb(es, "wkvb", [128, 4, 256], BF16)
                KT, B_KT = sb(es, "KT", [128, NK], BF16)
                sqK, B_sqK = sb(es, "sqK", [128, 512], BF16)
                Vh, B_Vh = sb(es, "Vh", [128, NKT, 128], BF16)
                scl, B_scl = sb(es, "scl", [128, NKT], F32)
                QTn, B_QTn = sb(es, "QTn", [128, 2048], BF16)
                QTr, B_QTr = sb(es, "QTr", [64, 2048], BF16)
                gqk, B_gqk = sb(es, "gqk", [128, QK], F32)
                gk1, B_gk1 = sb(es, "gk1", [128, QK], F32)
                ropeq, B_ropeq = sb(es, "ropeq", [128, 16, 64], F32)
                qh, B_qh = sb(es, "qh", [128, QK], F32)
                qb, B_qb = sb(es, "qb", [128, QK], BF16)
                ra, B_ra = sb(es, "ra2", [128, 2, 16], F32)
                rb, B_rb = sb(es, "rb2", [128, 2, 16], F32)
                PT = [sb(es, f"PT{i}", [128, 512], BF16) for i in range(2)]
                rec, B_rec = sb(es, "rec", [128, 512], F32)
                yat, B_yat = sb(es, "yat", [128, 512], BF16)
                k.dma("sp", gqk[:], gq.partition_broadcast(128), writes=[B_gqk])
                k.dma("sp", gk1[:], gk.partition_broadcast(128), writes=[B_gk1])
                k.dma("sp", ropeq[:].rearrange("p a c -> p (a c)"), cn["ropeq"], writes=[B_ropeq])
                k.op(dve, lambda e: e.tensor_tensor(out=gqk[:, 0:128], in0=gqk[:, 0:128], in1=gk1[:, 0:128], op=ALU.mult), reads=[B_gqk, B_gk1], writes=[B_gqk])
                for h in range(NH):
                    cast_load(wqb, w_qb[:, h * QK:(h + 1) * QK].rearrange("(kt p) n -> p kt n", p=128), B_wqb)
                    cast_load(wkvb, w_kvb[:, h * 256:(h + 1) * 256].rearrange("(kt p) n -> p kt n", p=128), B_wkvb)
                    for kc in range(17):
                        w_ = 512 if kc < 16 else 256
                        c0 = kc * 512
                        ps, B_ps = bank()
                        for kt in range(4):
                            k.op(pe, lambda e: e.matmul(ps[:, 0:w_], lhsT=wkvb[:, kt, 0:128], rhs=kvnT[:, kt, c0:c0 + w_], start=(kt == 0), stop=(kt == 3)),
                                 reads=[B_wkvb, B_kvnT], writes=[B_ps], inc=(kt == 3))
                        k.op(act, lambda e: e.copy(out=KT[:, c0:c0 + w_], in_=ps[:, 0:w_]), reads=[B_ps], writes=[B_KT])
                        k.op(act, lambda e: e.activation(out=sqK[:, 0:w_], in_=ps[:, 0:w_], func=AF.Square), reads=[B_ps], writes=[B_sqK])
                        for l_ in range(w_ // 128):
                            ti = kc * 4 + l_
                            k.op(pe, lambda e: e.matmul(psSS[:, ti:ti + 1], lhsT=sqK[:, l_ * 128:(l_ + 1) * 128], rhs=onesb[:, 0:1], start=True, stop=True),
                                 reads=[B_sqK, B_onesb], writes=[B_psSS])
                    k.op(dve, lambda e: e.scalar_tensor_tensor(out=scl[:], in0=psSS[:, 0:NKT], scalar=QK * EPS, in1=ssr[:], op0=ALU.add, op1=ALU.add), reads=[B_psSS, B_ssr], writes=[B_scl])
                    k.op(act, lambda e: e.activation(out=scl[:], in_=scl[:], func=AF.Sqrt), reads=[B_scl], writes=[B_scl])
                    k.op(dve, lambda e: e.reciprocal(out=scl[:], in_=scl[:]), reads=[B_scl], writes=[B_scl])
                    for vt in range(0, NKT, 4):
                        nl = min(4, NKT - vt)
                        ps, B_ps = bank()
                        for l_ in range(nl):
                            for kt in range(4):
                                k.op(pe, lambda e: e.matmul(ps[:, l_ * 128:(l_ + 1) * 128], lhsT=kvnT[:, kt, (vt + l_) * 128:(vt + l_ + 1) * 128], rhs=wkvb[:, kt, 128:256],
                                                            start=(kt == 0), stop=(kt == 3)), reads=[B_kvnT, B_wkvb], writes=[B_ps], inc=(kt == 3 and l_ == nl - 1))
                        k.op(act, lambda e: e.copy(out=Vh[:, vt:vt + nl, :], in_=ps[:, 0:nl * 128].rearrange("p (a d) -> p a d", d=128)), reads=[B_ps], writes=[B_Vh])
                    for m in range(16):
                        ps, B_ps = bank()
                        for kt in range(8):
                            k.op(pe, lambda e: e.matmul(ps[:, 0:QK], lhsT=qnT[:, kt, m * 128:(m + 1) * 128], rhs=wqb[:, kt, :], start=(kt == 0), stop=(kt == 7)),
                                 reads=[B_qnT, B_wqb], writes=[B_ps], inc=(kt == 7))
                        k.op(act, lambda e: e.activation(out=qh[:], in_=ps[:, 0:QK], func=AF.Square, accum_out=ss[:, 0:1]), reads=[B_ps], writes=[B_qh, B_ss])
                        k.op(dve, lambda e: e.tensor_scalar(out=rstd[:], in0=ss[:, 0:1], scalar1=1.0 / QK, scalar2=EPS, op0=ALU.mult, op1=ALU.add), reads=[B_ss], writes=[B_rstd])
                        k.op(act, lambda e: e.activation(out=rstd[:], in_=rstd[:], func=AF.Sqrt), reads=[B_rstd], writes=[B_rstd])
                        k.op(dve, lambda e: e.reciprocal(out=rstd[:], in_=rstd[:]), reads=[B_rstd], writes=[B_rstd])
                        k.op(dve, lambda e: e.scalar_tensor_tensor(out=qh[:], in0=ps[:, 0:QK], scalar=rstd[:, 0:1], in1=gqk[:], op0=ALU.mult, op1=ALU.mult), reads=[B_ps, B_rstd, B_gqk], writes=[B_qh])
                        k.op(act, lambda e: e.copy(out=qb[:, 0:128], in_=qh[:, 0:128]), reads=[B_qh], writes=[B_qb])
                        v4 = qh[:, 128:192].rearrange("p (a h f) -> p a h f", a=2, h=2)
                        o4 = qb[:, 128:192].rearrange("p (a h f) -> p a h f", a=2, h=2)
                        cs_ = ropeq[:, m, 0:32].rearrange("p (a f) -> p a f", a=2)
                        sn_ = ropeq[:, m, 32:64].rearrange("p (a f) -> p a f", a=2)
                        x1_, x2_ = v4[:, :, 0, :], v4[:, :, 1, :]
                        k.op(dve, lambda e: e.tensor_tensor(out=ra[:], in0=x1_, in1=cs_, op=ALU.mult), reads=[B_qh, B_ropeq], writes=[B_ra])
                        k.op(dve, lambda e: e.tensor_tensor(out=rb[:], in0=x2_, in1=sn_, op=ALU.mult), reads=[B_qh, B_ropeq], writes=[B_rb])
                        k.op(dve, lambda e: e.tensor_tensor(out=o4[:, :, 0, :], in0=ra[:], in1=rb[:], op=ALU.subtract), reads=[B_ra, B_rb], writes=[B_qb])
                        k.op(dve, lambda e: e.tensor_tensor(out=ra[:], in0=x1_, in1=sn_, op=ALU.mult), reads=[B_qh, B_ropeq], writes=[B_ra])
                        k.op(dve, lambda e: e.tensor_tensor(out=rb[:], in0=x2_, in1=cs_, op=ALU.mult), reads=[B_qh, B_ropeq], writes=[B_rb])
                        k.op(dve, lambda e: e.tensor_tensor(out=o4[:, :, 1, :], in0=ra[:], in1=rb[:], op=ALU.add), reads=[B_ra, B_rb], writes=[B_qb])
                        hb = m % 2
                        k.op(pe, lambda e: e.transpose(psT2[hb][:, 0:128], qb[:, 0:128], identb[:]), reads=[B_qb, B_identb], writes=[B_psT[hb]], inc=False)
                        k.op(pe, lambda e: e.transpose(psT2[hb][0:64, 128:256], qb[:, 128:192], identb[:]), reads=[B_qb, B_identb], writes=[B_psT[hb]])
                        k.op(act, lambda e: e.copy(out=QTn[:, m * 128:(m + 1) * 128], in_=psT2[hb][:, 0:128]), reads=[B_psT[hb]], writes=[B_QTn])
                        k.op(act, lambda e: e.copy(out=QTr[0:64, m * 128:(m + 1) * 128], in_=psT2[hb][0:64, 128:256]), reads=[B_psT[hb]], writes=[B_QTr])
                    for qc in range(4):
                        qs = slice(qc * 512, (qc + 1) * 512)
                        for kt in range(NKT):
                            ps, B_ps = bank()
                            pt, B_pt = PT[kt % 2]
                            k.op(pe, lambda e: e.matmul(ps[:], lhsT=KT[:, kt * 128:(kt + 1) * 128], rhs=QTn[:, qs], start=True, stop=False), reads=[B_KT, B_QTn], writes=[B_ps], inc=False)
                            k.op(pe, lambda e: e.matmul(ps[:], lhsT=krT[0:64, kt * 128:(kt + 1) * 128], rhs=QTr[0:64, qs], start=False, stop=True), reads=[B_krT, B_QTr], writes=[B_ps])
                            k.op(act, lambda e: e.activation(out=pt[:], in_=ps[:], func=AF.Exp, scale=scl[:, kt:kt + 1]), reads=[B_ps, B_scl], writes=[B_pt])
                            k.op(pe, lambda e: e.matmul(psO[:], lhsT=Vh[:, kt, :], rhs=pt[:], start=(kt == 0), stop=(kt == NKT - 1)), reads=[B_Vh, B_pt], writes=[B_psO], inc=False)
                            k.op(pe, lambda e: e.matmul(psL[:], lhsT=onesb[:], rhs=pt[:], start=(kt == 0), stop=(kt == NKT - 1)), reads=[B_onesb, B_pt], writes=[B_psL])
                        k.op(dve, lambda e: e.reciprocal(out=rec[:], in_=psL[:]), reads=[B_psL], writes=[B_rec])
                        k.op(dve, lambda e: e.tensor_tensor(out=yat[:], in0=psO[:], in1=rec[:], op=ALU.mult), reads=[B_psO, B_rec], writes=[B_yat])
                        k.dma("sp", yT_d[HYW + h * 128: HYW + (h + 1) * 128, qs], yat[:], reads=[B_yat], writes=[B_yT])
                bankmod[0] = 6
                bankrr[0] = 0
        k.barrier()
        if stop_after == "E":
            o = nc.dram_tensor("dbg_yT", [D, 2048], BF16, kind="ExternalOutput").ap()
            with ExitStack() as es:
                t_, B_t = sb(es, "dbg_t", [128, 32, 2048], BF16)
                k.dma("sp", t_[:], yT_d.rearrange("(k p) t -> p k t", p=128), reads=[B_yT], writes=[B_t])
                k.dma("sp", o.rearrange("(k p) t -> p k t", p=128), t_[:], reads=[B_t], is_output=True)
            k.finish()
            return nc

        with ExitStack() as es:
            g1row, B_g1 = sb(es, "g1row", [128, D], F32)
            yTs, B_yTs = sb(es, "yTs", [128, 32, 1024], BF16)
            wo = [sb(es, f"wo{i}", [128, 32, 512], BF16) for i in range(2)]
            xin = [sb(es, f"xin{i}", [128, 512], F32) for i in range(2)]
            xo = [sb(es, f"xo{i}", [128, 512], F32) for i in range(2)]
            k.dma("sp", g1row[:], mod_d[0:1, 2 * D:3 * D].partition_broadcast(128), reads=[B_mod], writes=[B_g1])
            cnt = 0
            for half in range(2):
                for kt in range(32):
                    k.dma("sp", yTs[:, kt, :], yT_d[kt * 128:(kt + 1) * 128, half * 1024:(half + 1) * 1024], reads=[B_yT], writes=[B_yTs])
                for f_ in range(8):
                    wt, B_wt = wo[f_ % 2]
                    fs = slice(f_ * 512, (f_ + 1) * 512)
                    cast_load(wt, w_out[:, fs].rearrange("(kt p) n -> p kt n", p=128), B_wt)
                    for tt_ in range(8):
                        r0 = half * 1024 + tt_ * 128
                        xi, B_xi = xin[cnt % 2]
                        xo_, B_xo = xo[cnt % 2]
                        cnt += 1
                        k.dma("sp", xi[:], x_own[r0:r0 + 128, fs], writes=[B_xi])
                        ps, B_ps = bank()
                        for kt in range(32):
                            k.op(pe, lambda e: e.matmul(ps[:], lhsT=yTs[:, kt, tt_ * 128:(tt_ + 1) * 128], rhs=wt[:, kt, :], start=(kt == 0), stop=(kt == 31)),
                                 reads=[B_yTs, B_wt], writes=[B_ps], inc=(kt == 31))
                        k.op(dve, lambda e: e.tensor_tensor(out=xo_[:], in0=ps[:], in1=g1row[:, fs], op=ALU.mult), reads=[B_ps, B_g1], writes=[B_xo])
                        k.op(pool, lambda e: e.tensor_tensor(out=xo_[:], in0=xo_[:], in1=xi[:], op=ALU.add), reads=[B_xo, B_xi], writes=[B_xo])
                        k.dma("sp", xnew_d[r0:r0 + 128, fs], xo_[:], reads=[B_xo], writes=[B_xnew])
        k.barrier()
        if stop_after == "F":
            for i_ in range(16):
                k.dma("sp", out[i_ * 128:(i_ + 1) * 128, :], xnew_d[i_ * 128:(i_ + 1) * 128, :], reads=[B_xnew], is_output=True)
            k.finish()
            return nc

        with ExitStack() as es:
            h2T, B_h2T = sb(es, "h2T", [128, 32, 512], BF16)
            acc = es.enter_context(nc.sbuf_tensor("acc", [128, 4, D], F32))
            B_acc = [Buf(f"acc{i}") for i in range(4)]
            aT, B_aT = sb(es, "aT", [128, 16, 512], BF16)
            w1t = [sb(es, f"w1t{i}", [128, 32, 128], BF16) for i in range(2)]
            w2c = [sb(es, f"w2c{i}", [128, 16, 256], BF16) for i in range(2)]
            xs, B_xs = sb(es, "xs_m", [128, D], BF16)
            rl, B_rl = sb(es, "rl", [128, 512], F32)
            ss1, B_ss1 = sb(es, "ss1m", [128, 1], F32)
            rstd, B_rstd = sb(es, "rstd_m", [128, 1], F32)
            g2p, B_g2p = sb(es, "g2p", [128, 512], F32)
            xin, B_xin = sb(es, "xin_m", [128, 512], F32)
            oo, B_oo = sb(es, "oo", [128, 512], F32)
            n1 = 0
            n2_ = 0
            for ck in range(4):
                for tt_ in range(4):
                    r0 = ck * 512 + tt_ * 128
                    k.dma("sp", acc[:, tt_, :], xnew_d[r0:r0 + 128, :], reads=[B_xnew], writes=[B_acc[tt_]])
                    make_hT(None, acc[:, tt_, :], B_acc[tt_], h2T[:, :, tt_ * 128:(tt_ + 1) * 128], B_h2T, 4, 5, (xs, B_xs, ss1, B_ss1, rstd, B_rstd))
                for fc in range(8):
                    for ft in range(16):
                        ff = fc * 2048 + ft * 128
                        wt, B_wt = w1t[n1 % 2]
                        n1 += 1
                        cast_load(wt, w_mlp1[:, ff:ff + 128].rearrange("(kt p) n -> p kt n", p=128), B_wt, step=8)
                        ps, B_ps = bank()
                        for kt in range(32):
                            k.op(pe, lambda e: e.matmul(ps[:], lhsT=wt[:, kt, :], rhs=h2T[:, kt, :], start=(kt == 0), stop=(kt == 31)), reads=[B_wt, B_h2T], writes=[B_ps], inc=(kt == 31))
                        k.op(act, lambda e: e.activation(out=rl[:], in_=ps[:], func=AF.Relu), reads=[B_ps], writes=[B_rl])
                        k.op(dve, lambda e: e.tensor_tensor(out=aT[:, ft, :], in0=rl[:], in1=rl[:], op=ALU.mult), reads=[B_rl], writes=[B_aT])
                    for fo in range(16):
                        wt, B_wt = w2c[n2_ % 2]
                        n2_ += 1
                        os_ = slice(fo * 256, (fo + 1) * 256)
                        cast_load(wt, w_mlp2[fc * 2048:(fc + 1) * 2048, os_].rearrange("(ft p) n -> p ft n", p=128), B_wt, step=8)
                        for tt_ in range(4):
                            ps, B_ps = bank()
                            for ft in range(16):
                                k.op(pe, lambda e: e.matmul(ps[:, 0:256], lhsT=aT[:, ft, tt_ * 128:(tt_ + 1) * 128], rhs=wt[:, ft, :], start=(ft == 0), stop=(ft == 15)),
                                     reads=[B_aT, B_wt], writes=[B_ps], inc=(ft == 15))
                            if fc == 0:
                                k.op(act, lambda e: e.copy(out=acc[:, tt_, os_], in_=ps[:, 0:256]), reads=[B_ps], writes=[B_acc[tt_]])
                            else:
                                k.op(dve, lambda e: e.tensor_tensor(out=acc[:, tt_, os_], in0=acc[:, tt_, os_], in1=ps[:, 0:256], op=ALU.add), reads=[B_ps, B_acc[tt_]], writes=[B_acc[tt_]])
                for f8 in range(8):
                    fs = slice(f8 * 512, (f8 + 1) * 512)
                    k.dma("sp", g2p[:], mod_d[0:1, 5 * D + f8 * 512: 5 * D + (f8 + 1) * 512].partition_broadcast(128), reads=[B_mod], writes=[B_g2p])
                    for tt_ in range(4):
                        r0 = ck * 512 + tt_ * 128
                        k.dma("sp", xin[:], xnew_d[r0:r0 + 128, fs], reads=[B_xnew], writes=[B_xin])
                        k.op(dve, lambda e: e.tensor_tensor(out=oo[:], in0=acc[:, tt_, fs], in1=g2p[:], op=ALU.mult), reads=[B_acc[tt_], B_g2p], writes=[B_oo])
                        k.op(pool, lambda e: e.tensor_tensor(out=oo[:], in0=oo[:], in1=xin[:], op=ALU.add), reads=[B_oo, B_xin], writes=[B_oo])
                        k.dma("sp", out[r0:r0 + 128, fs], oo[:], reads=[B_oo], is_output=True)
        k.finish()
        return nc


def _chan_groups(j, NG):
    if NG == NG_ALL:
        return [np.arange(g * 64, (g + 1) * 64) for g in range(NG)]
    return [np.arange(512 * j + g * 64, 512 * j + (g + 1) * 64) for g in range(NG)]


def make_in_map(inp, core, NG=NG_ALL):
    b, j = core // 4, core % 4
    f = lambda a: np.ascontiguousarray(np.asarray(a, dtype=np.float32))
    groups = _chan_groups(j, NG)
    w_in = np.asarray(inp["w_in"])[0]
    hcols = np.concatenate([np.concatenate([g, HYW + g, 2 * HYW + g]) for g in groups])
    w3 = np.asarray(inp["hy_filt_w3"])[0].reshape(64, 2, 2, HYW)
    w3g = np.concatenate([w3[:, :, :, g].reshape(64, 2, 2, 2, 32).transpose(0, 2, 3, 1, 4).reshape(64, 256) for g in groups], 1)
    hb = np.asarray(inp["hy_bias"])[0]
    hbg = np.concatenate([hb[:, g].reshape(-1) for g in groups])[None, :]
    m = {
        "x": f(inp["x"][b]), "x_own": f(inp["x"][b][2048 * j:2048 * (j + 1)]), "ctx": f(inp["ctx"][b]),
        "cvec": f(np.stack([np.asarray(inp["c"])[b], np.asarray(inp["c_ctx"])])),
        "norm_g": f(np.stack([np.asarray(inp["norm1_g"])[0], np.asarray(inp["norm2_g"])[0]])),
        "w_ada": f(inp["w_ada"][0]), "b_ada": f(np.asarray(inp["b_ada"])[0][None, :]),
        "w_hy": f(w_in[:, hcols]), "w_q": f(w_in[:, HYC:HYC + QL]), "w_kv": f(w_in[:, HYC + QL:]),
        "hy_cw": f(np.asarray(inp["hy_conv_w"])[0][:, hcols]), "hy_cb": f(np.asarray(inp["hy_conv_b"])[0][hcols][None, :]),
        "f_w1": f(inp["hy_filt_w1"][0]), "f_b1": f(np.asarray(inp["hy_filt_b1"])[0][:, None]), "f_w2": f(inp["hy_filt_w2"][0]),
        "f_b2": f(np.asarray(inp["hy_filt_b2"])[0][:, None]), "f_freq": f(np.asarray(inp["hy_freq"])[0][:, None]),
        "f_w3": f(w3g), "hy_bias": f(hbg),
        "g_qa": f(np.asarray(inp["mla_g_qa"])[0][None, :]), "g_kva": f(np.asarray(inp["mla_g_kva"])[0][None, :]),
        "w_qb": f(inp["mla_w_qb"][0]), "w_kvb": f(inp["mla_w_kvb"][0]),
        "gq": f(np.asarray(inp["mla_q_norm_g"])[0][None, :]), "gk": f(np.asarray(inp["mla_k_norm_g"])[0][None, :]),
        "w_out": f(inp["w_out"][0]), "w_mlp1": f(inp["w_mlp1"][0]), "w_mlp2": f(inp["w_mlp2"][0]),
    }
    c = _consts(j)
    c["deltas"] = np.ascontiguousarray(np.concatenate([c["deltas"][0, g] for g in groups])[None, :])
    for kname, v in c.items():
        m["c_" + kname] = np.ascontiguousarray(v.astype(np.float32))
    return m


def kernel(**inputs):
    nc = build_program()
    in_maps = [make_in_map(inputs, core) for core in range(8)]
    res = run_bass_kernel_spmd(nc, in_maps, core_ids=list(range(8)))
    outp = np.zeros((NB, L, D), np.float32)
    for core in range(8):
        b, j = core // 4, core % 4
        outp[b, 2048 * j:2048 * (j + 1)] = res.results[core]["out"]
    return outp
```

```python
import math
import numpy as np
import concourse.bass as bass
import concourse.mybir as mybir
from concourse.bass_utils import run_bass_kernel_spmd

F32 = mybir.dt.float32
BF16 = mybir.dt.bfloat16
AF = mybir.ActivationFunctionType
ALU = mybir.AluOpType

D = 4096
L = 8192
LC = 256
NB = 2
HYW = 2048
HYC = 6144
QL = 1024
KVL = 512
ROPE = 64
NOPE = 128
QK = 192
DV = 128
NH = 16
DFF = 16384
INC = 7744
EPS = 1e-6
NFFT = 16384
PI = math.pi


class Buf:
    __slots__ = ("w", "r", "name")

    def __init__(self, name=""):
        self.w = None
        self.r = {}
        self.name = name


class Eng:
    def __init__(self, name, handle, sem, is_pe=False):
        self.name = name
        self.h = handle
        self.sem = sem
        self.count = 0
        self.ninst = 0
        self.waited = {}
        self.is_pe = is_pe


class K:
    def __init__(self, nc, sems, n_sp=20, n_pool=12):
        self.nc = nc
        it = iter(sems)
        self.pe = Eng("pe", nc.tensor, next(it), True)
        self.act = Eng("act", nc.scalar, next(it))
        self.dve = Eng("dve", nc.vector, next(it))
        self.pool = Eng("pool", nc.gpsimd, next(it))
        self.sp = Eng("sp", nc.sync, None)
        self.q = {
            "sp": dict(eng=self.sp, sems=[next(it) for _ in range(n_sp)], tot=[0] * n_sp, i=0),
            "pool": dict(eng=self.pool, sems=[next(it) for _ in range(n_pool)], tot=[0] * n_pool, i=0),
        }
        self.out_events = []

    def _wait(self, E, ev):
        if ev is None:
            return
        sem, val, src = ev
        if src is E and E.is_pe:
            return
        if E.waited.get(id(sem), 0) >= val:
            return
        E.h.wait_ge(sem, val)
        E.waited[id(sem)] = val
        E.ninst += 1

    def _deps(self, E, reads, writes):
        for b in reads:
            self._wait(E, b.w)
        for b in writes:
            self._wait(E, b.w)
            for ev in list(b.r.values()):
                self._wait(E, ev)

    def _mark(self, ev, key, reads, writes):
        for b in reads:
            b.r[key] = ev
        for b in writes:
            b.w = ev
            b.r = {}

    def op(self, E, fn, reads=(), writes=(), inc=True):
        self._deps(E, reads, writes)
        ins = fn(E.h)
        E.ninst += 1
        val = E.count + 1
        if inc:
            ins.then_inc(E.sem, 1)
            E.count = val
        self._mark((E.sem, val, E), id(E.sem), reads, writes)
        return ins

    def dma(self, qn, out, in_, reads=(), writes=(), is_output=False):
        Q = self.q[qn]
        E = Q["eng"]
        i = Q["i"]
        Q["i"] = (i + 1) % len(Q["sems"])
        sem = Q["sems"][i]
        if Q["tot"][i] > 0:
            self._wait(E, (sem, Q["tot"][i], None))
        self._deps(E, reads, writes)
        ins = E.h.dma_start(out=out, in_=in_)
        E.ninst += 1
        Q["tot"][i] += 16
        ins.then_inc(sem, 16)
        ev = (sem, Q["tot"][i], None)
        self._mark(ev, id(sem), reads, writes)
        if is_output:
            self.out_events.append(ev)
        return ins

    def barrier(self):
        engs = (self.pe, self.act, self.dve, self.pool, self.sp)
        for E in engs:
            for Q in self.q.values():
                for sem, tot in zip(Q["sems"], Q["tot"]):
                    if tot > 0:
                        self._wait(E, (sem, tot, None))
            for F in (self.pe, self.act, self.dve, self.pool):
                if F is not E and F.count > 0:
                    self._wait(E, (F.sem, F.count, F))

    def stats(self):
        return {E.name: E.ninst for E in (self.pe, self.act, self.dve, self.pool, self.sp)}

    def finish(self):
        for Q in self.q.values():
            for sem, tot in zip(Q["sems"], Q["tot"]):
                if tot > 0:
                    self._wait(self.sp, (sem, tot, None))
        for E in (self.pe, self.act, self.dve, self.pool):
            if E.count > 0:
                self._wait(self.sp, (E.sem, E.count, E))


def _consts(j):
    c = {}
    c["ident"] = np.eye(128, dtype=np.float32)
    n1 = np.arange(128)[:, None].astype(np.float64)
    k1 = np.arange(256)[None, :].astype(np.float64)
    ang = 2 * np.pi * n1 * k1 / 256.0
    c["f256"] = np.concatenate([np.cos(ang), -np.sin(ang)], 1).astype(np.float32)
    n2 = (np.arange(128) % 64)[:, None].astype(np.float64)
    th = 2 * np.pi * n2 * k1 / NFFT
    c["twc"] = np.cos(th).astype(np.float32)
    c["tws"] = np.sin(th).astype(np.float32)
    a = np.arange(64)[:, None] * np.arange(64)[None, :] * (2 * np.pi / 64.0)
    C = np.zeros((128, 128)); S = np.zeros((128, 128))
    for q in range(2):
        C[q * 64:(q + 1) * 64, q * 64:(q + 1) * 64] = np.cos(a)
        S[q * 64:(q + 1) * 64, q * 64:(q + 1) * 64] = np.sin(a)
    c["c64"] = C.astype(np.float32); c["s64"] = S.astype(np.float32); c["ns64"] = (-S).astype(np.float32)
    kk = np.arange(256)[:, None].astype(np.float64); nn = np.arange(128)[None, :].astype(np.float64)
    ph = 2 * np.pi * kk * nn / 256.0
    ic = (np.cos(ph) / NFFT).reshape(2, 128, 128); isn = (-np.sin(ph) / NFFT).reshape(2, 128, 128)
    c["icis"] = np.stack([ic, isn], 1).transpose(2, 0, 1, 3).reshape(128, 4 * 128).astype(np.float32)
    f32 = np.float32
    pos = np.arange(L, dtype=f32)
    t = np.linspace(0.0, 1.0, L, dtype=f32)
    bands = np.linspace(1e-4, 15, 16, dtype=f32)
    angp = (f32(2.0 * math.pi / L) * pos[:, None]) * bands[None, :]
    feats = np.concatenate([t[:, None], np.cos(angp), -np.sin(angp)], -1).astype(f32)
    c["featsT"] = np.ascontiguousarray(feats.T)
    c["negt"] = np.ascontiguousarray(-t.reshape(128, 64))
    deltas = np.abs(np.linspace(math.log(1e-2) / 1.5, math.log(1e-2) / 0.3, HYW, dtype=f32))
    c["deltas"] = deltas.reshape(1, HYW).astype(f32)
    half = ROPE // 2
    inv = (10000.0 ** (-np.arange(0, half, 2, dtype=f32) / half)).astype(f32)
    tok = np.arange(L)
    row = (tok // 64).astype(f32); col = (tok % 64).astype(f32)
    angr = np.concatenate([row[:, None] * inv, col[:, None] * inv], -1).astype(f32)
    cs, sn = np.cos(angr), np.sin(angr)
    idx = (64 * np.arange(128)[:, None] + np.arange(64)[None, :])
    c["ropek"] = np.concatenate([cs[idx], sn[idx]], -1).reshape(128, 64 * 64).astype(f32)
    own = 2048 * j + np.arange(2048).reshape(16, 128).T
    c["ropeq"] = np.concatenate([cs[own], sn[own]], -1).reshape(128, 16 * 64).astype(f32)
    sel = np.zeros((128, 32), np.float32)
    sel[32 * j + np.arange(32), np.arange(32)] = 1.0
    c["sel"] = sel
    c["ones"] = np.ones((128, 128), np.float32)
    c["zeros"] = np.zeros((1, 256), np.float32)
    return c


MAGIC = 12582912.0
NG_ALL = 32


def build_program(NG=NG_ALL, stop_after=None, dbg=False):
    from contextlib import ExitStack
    nc = bass.Bass("TRN2", target_bir_lowering=False)

    def din(name, shape, dt=F32):
        return nc.dram_tensor(name, list(shape), dt, kind="ExternalInput").ap()

    x = din("x", [L, D]); x_own = din("x_own", [2048, D]); ctx = din("ctx", [LC, D])
    cvec = din("cvec", [2, D]); norm_g = din("norm_g", [2, D])
    w_ada = din("w_ada", [D, 6 * D]); b_ada = din("b_ada", [1, 6 * D])
    w_hy = din("w_hy", [D, NG * 192]); w_q = din("w_q", [D, QL]); w_kv = din("w_kv", [D, 576])
    hy_cw = din("hy_cw", [3, NG * 192]); hy_cb = din("hy_cb", [1, NG * 192])
    f_w1 = din("f_w1", [33, 64]); f_b1 = din("f_b1", [64, 1]); f_w2 = din("f_w2", [64, 64]); f_b2 = din("f_b2", [64, 1])
    f_freq = din("f_freq", [64, 1]); f_w3 = din("f_w3", [64, NG * 256]); hy_bias = din("hy_bias", [1, NG * 128])
    g_qa = din("g_qa", [1, QL]); g_kva = din("g_kva", [1, KVL]); w_qb = din("w_qb", [QL, NH * QK]); w_kvb = din("w_kvb", [KVL, NH * 256])
    gq = din("gq", [1, QK]); gk = din("gk", [1, QK])
    tiny = stop_after is not None
    w_out = din("w_out", [D, D]); w_mlp1 = din("w_mlp1", [128, 128] if tiny else [D, DFF]); w_mlp2 = din("w_mlp2", [128, 128] if tiny else [DFF, D])
    cn = {}
    for nm, shp in [("ident", [128, 128]), ("f256", [128, 512]), ("twc", [128, 256]), ("tws", [128, 256]), ("c64", [128, 128]),
                    ("s64", [128, 128]), ("ns64", [128, 128]), ("icis", [128, 512]), ("featsT", [33, L]), ("negt", [128, 64]),
                    ("deltas", [1, NG * 64]), ("ropek", [128, 4096]), ("ropeq", [128, 1024]), ("sel", [128, 32]), ("ones", [128, 128]),
                    ("zeros", [1, 256])]:
        cn[nm] = din("c_" + nm, shp)
    out = nc.dram_tensor("out", [2048, D], F32, kind="ExternalOutput").ap()
    dbg_out = {}

    def dram(name, shape, dt):
        return nc.dram_tensor(name, list(shape), dt, kind="Internal").ap()

    mod_d = dram("mod_d", [2, 6 * D], F32)
    hT_d = dram("hT_d", [64, 128, D], BF16)
    yT_d = dram("yT_d", [D, 2048], BF16)
    xnew_d = dram("xnew_d", [2048, D], F32)
    B_mod, B_hT, B_yT, B_xnew = Buf("mod_d"), [Buf() for _ in range(64)], Buf("yT_d"), Buf("xnew_d")

    with ExitStack() as es0:
        sems = [es0.enter_context(nc.semaphore(f"s{i}")) for i in range(4 + 20 + 12)]
        k = K(nc, sems)
        pe, act, dve, pool = k.pe, k.act, k.dve, k.pool

        def sb(es, name, shape, dt=F32):
            return es.enter_context(nc.sbuf_tensor(name, list(shape), dt)), Buf(name)


        def cast_load(dst, src, B_dst, step=4):
            n = dst.shape[1]
            for a in range(0, n, step):
                k.dma("pool", dst[:, a:a + step, :], src[:, a:a + step, :], writes=[B_dst])
        banks = []
        for i in range(6):
            banks.append((es0.enter_context(nc.psum_tensor(f"bank{i}", [128, 512], F32)), Buf(f"bank{i}")))
        bankrr = [0]
        bankmod = [6]

        def bank():
            i = bankrr[0]
            bankrr[0] = (i + 1) % bankmod[0]
            return banks[i]
        psT2 = [es0.enter_context(nc.psum_tensor(f"psT{i}", [128, 1024], BF16)) for i in range(2)]
        B_psT = [Buf("psT0"), Buf("psT1")]

        identb, B_identb = sb(es0, "identb", [128, 128], BF16)
        identf, B_identf = sb(es0, "identf", [128, 128], F32)
        onesb, B_onesb = sb(es0, "onesb", [128, 128], BF16)
        k.dma("pool", identb[:], cn["ident"], writes=[B_identb])
        k.dma("sp", identf[:], cn["ident"], writes=[B_identf])
        k.dma("pool", onesb[:], cn["ones"], writes=[B_onesb])
        mcol, B_mcol = sb(es0, "mcol", [128, 8, 32], F32)

        def rms_rstd(E_sq, src_ap, src_bufs, width, ss, B_ss, junk, B_junk, rstd, B_rstd):
            k.op(act, lambda e: e.activation(out=junk, in_=src_ap, func=AF.Square, accum_out=ss), reads=src_bufs, writes=[B_junk, B_ss])
            k.op(dve, lambda e: e.tensor_scalar(out=rstd, in0=ss, scalar1=1.0 / width, scalar2=EPS, op0=ALU.mult, op1=ALU.add), reads=[B_ss], writes=[B_rstd])
            k.op(act, lambda e: e.activation(out=rstd, in_=rstd, func=AF.Sqrt), reads=[B_rstd], writes=[B_rstd])
            k.op(dve, lambda e: e.reciprocal(out=rstd, in_=rstd), reads=[B_rstd], writes=[B_rstd])

        with ExitStack() as es:
            cs, B_cs = sb(es, "cs", [128, 2, 32], F32)
            wa = [sb(es, f"wa{i}", [128, 32, 512], F32) for i in range(2)]
            brow, B_brow = sb(es, "brow", [2, 512], F32)
            mrow, B_mrow = sb(es, "mrow", [2, 512], F32)
            for r in range(2):
                k.dma("sp", cs[:, r, :], cvec[r:r + 1, :].rearrange("o (p k) -> p (o k)", k=32), writes=[B_cs])
            k.op(act, lambda e: e.activation(out=cs[:], in_=cs[:], func=AF.Silu), reads=[B_cs], writes=[B_cs])
            wav = w_ada.rearrange("(p k) n -> p k n", k=32)
            for ch in range(48):
                wt, B_wt = wa[ch % 2]
                k.dma("sp", wt[:], wav[:, :, ch * 512:(ch + 1) * 512], writes=[B_wt])
                for r in range(2):
                    k.dma("sp", brow[r:r + 1, :], b_ada[0:1, ch * 512:(ch + 1) * 512], writes=[B_brow])
                ps, B_ps = bank()
                for kk in range(32):
                    k.op(pe, lambda e: e.matmul(ps[0:2, :], lhsT=cs[:, :, kk], rhs=wt[:, kk, :], start=(kk == 0), stop=(kk == 31)),
                         reads=[B_cs, B_wt], writes=[B_ps], inc=(kk == 31))
                k.op(dve, lambda e: e.tensor_tensor(out=mrow[:], in0=ps[0:2, :], in1=brow[:], op=ALU.add), reads=[B_ps, B_brow], writes=[B_mrow])
                k.dma("sp", mod_d[:, ch * 512:(ch + 1) * 512], mrow[:], reads=[B_mrow], writes=[B_mod])
            va, B_va = sb(es, "va", [128, 128], F32)
            vb, B_vb = sb(es, "vb", [128, 128], F32)
            srcA = [mod_d[0:1, 0:D], mod_d[0:1, D:2 * D], mod_d[1:2, 0:D], mod_d[1:2, D:2 * D]]
            srcB = [mod_d[0:1, 3 * D:4 * D], mod_d[0:1, 4 * D:5 * D], norm_g[0:1, :], norm_g[1:2, :]]
            for i in range(4):
                k.dma("sp", va[32 * i:32 * i + 32, :], srcA[i].rearrange("o (k p) -> (o k) p", p=128), reads=[B_mod], writes=[B_va])
                k.dma("sp", vb[32 * i:32 * i + 32, :], srcB[i].rearrange("o (k p) -> (o k) p", p=128), reads=[B_mod], writes=[B_vb])
            psa, B_psa = bank()
            k.op(pe, lambda e: e.transpose(psa[:, 0:128], va[:], identf[:]), reads=[B_va, B_identf], writes=[B_psa])
            k.op(pe, lambda e: e.transpose(psa[:, 128:256], vb[:], identf[:]), reads=[B_vb, B_identf], writes=[B_psa])
            k.op(dve, lambda e: e.tensor_copy(out=mcol[:].rearrange("p a k -> p (a k)"), in_=psa[:, 0:256]), reads=[B_psa], writes=[B_mcol])
            for (isc, ig) in [(1, 6), (3, 6), (5, 7)]:
                k.op(dve, lambda e: e.scalar_tensor_tensor(out=mcol[:, isc, :], in0=mcol[:, isc, :], scalar=1.0, in1=mcol[:, ig, :], op0=ALU.add, op1=ALU.mult),
                     reads=[B_mcol], writes=[B_mcol])
        k.barrier()
        if stop_after == "A":
            o = nc.dram_tensor("dbg_mod", [2, 6 * D], F32, kind="ExternalOutput").ap()
            t, B_t = sb(es0, "dbgt", [2, 6 * D], F32)
            k.dma("sp", t[:], mod_d, reads=[B_mod], writes=[B_t])
            k.dma("sp", o, t[:], reads=[B_t], is_output=True)
            o2 = nc.dram_tensor("dbg_mcol", [128, 256], F32, kind="ExternalOutput").ap()
            k.dma("sp", o2, mcol[:].rearrange("p a k -> p (a k)"), reads=[B_mcol], is_output=True)
            k.finish()
            return nc

        def make_hT(es_tmp, xt, B_xt, hT_ap, B_hT_, ish, iG, tmp):
            xs, B_xs, ss, B_ss, rstd, B_rstd = tmp
            k.op(act, lambda e: e.activation(out=xs[:], in_=xt, func=AF.Square, accum_out=ss[:]), reads=[B_xt], writes=[B_xs, B_ss])
            k.op(dve, lambda e: e.tensor_scalar(out=rstd[:], in0=ss[:], scalar1=1.0 / D, scalar2=EPS, op0=ALU.mult, op1=ALU.add), reads=[B_ss], writes=[B_rstd])
            k.op(act, lambda e: e.activation(out=rstd[:], in_=rstd[:], func=AF.Sqrt), reads=[B_rstd], writes=[B_rstd])
            k.op(dve, lambda e: e.reciprocal(out=rstd[:], in_=rstd[:]), reads=[B_rstd], writes=[B_rstd])
            k.op(dve, lambda e: e.tensor_scalar(out=xs[:], in0=xt, scalar1=rstd[:, 0:1], scalar2=None, op0=ALU.mult), reads=[B_xt, B_rstd], writes=[B_xs])
            for q4 in range(8):
                hb = q4 % 2
                for i in range(4):
                    kt = q4 * 4 + i
                    k.op(pe, lambda e: e.transpose(psT2[hb][:, i * 128:(i + 1) * 128], xs[:, kt * 128:(kt + 1) * 128], identb[:]),
                         reads=[B_xs, B_identb], writes=[B_psT[hb]], inc=(i == 3))
                for i in range(4):
                    kt = q4 * 4 + i
                    k.op(act, lambda e: e.activation(out=hT_ap[:, kt, :], in_=psT2[hb][:, i * 128:(i + 1) * 128], func=AF.Identity,
                                                     scale=mcol[:, iG, kt:kt + 1], bias=mcol[:, ish, kt:kt + 1]),
                         reads=[B_psT[hb], B_mcol], writes=[B_hT_])

        with ExitStack() as es_hy:
            hid2T, B_hid2 = sb(es_hy, "hid2T", [64, L], F32)
            fcol, B_fcol = sb(es_hy, "fcol", [64, 8], F32)
            with ExitStack() as es:
                featsT, B_ft = sb(es, "featsT", [33, L], F32)
                h1T, B_h1 = sb(es, "h1T", [64, L], F32)
                fw1, B_fw1 = sb(es, "fw1", [33, 64], F32)
                fw2, B_fw2 = sb(es, "fw2", [64, 64], F32)
                pre, B_pre = sb(es, "pre", [64, 512], F32)
                rr, B_rr = sb(es, "rr", [64, 512], F32)
                k.dma("sp", featsT[:], cn["featsT"], writes=[B_ft])
                k.dma("sp", fw1[:], f_w1, writes=[B_fw1])
                k.dma("sp", fw2[:], f_w2, writes=[B_fw2])
                k.dma("sp", fcol[:, 0:1], f_b1, writes=[B_fcol])
                k.dma("sp", fcol[:, 1:2], f_b2, writes=[B_fcol])
                k.dma("sp", fcol[:, 2:3], f_freq, writes=[B_fcol])
                k.op(dve, lambda e: e.tensor_tensor(out=fcol[:, 3:4], in0=fcol[:, 0:1], in1=fcol[:, 2:3], op=ALU.mult), reads=[B_fcol], writes=[B_fcol])
                k.op(dve, lambda e: e.tensor_tensor(out=fcol[:, 4:5], in0=fcol[:, 1:2], in1=fcol[:, 2:3], op=ALU.mult), reads=[B_fcol], writes=[B_fcol])
                for layer in range(2):
                    for chn in range(16):
                        cs_ = slice(chn * 512, (chn + 1) * 512)
                        ps, B_ps = bank()
                        if layer == 0:
                            k.op(pe, lambda e: e.matmul(ps[0:64, :], lhsT=fw1[0:33, :], rhs=featsT[0:33, cs_], start=True, stop=True), reads=[B_fw1, B_ft], writes=[B_ps])
                        else:
                            k.op(pe, lambda e: e.matmul(ps[0:64, :], lhsT=fw2[0:64, :], rhs=h1T[0:64, cs_], start=True, stop=True), reads=[B_fw2, B_h1], writes=[B_ps])
                        k.op(dve, lambda e: e.tensor_scalar(out=pre[:], in0=ps[0:64, :], scalar1=fcol[:, 2:3], scalar2=fcol[:, 3 + layer:4 + layer], op0=ALU.mult, op1=ALU.add),
                             reads=[B_ps, B_fcol], writes=[B_pre])
                        k.op(dve, lambda e: e.tensor_scalar(out=rr[:], in0=pre[:], scalar1=1.0 / (2 * PI), scalar2=MAGIC, op0=ALU.mult, op1=ALU.add), reads=[B_pre], writes=[B_rr])
                        k.op(dve, lambda e: e.tensor_scalar(out=rr[:], in0=rr[:], scalar1=-MAGIC, scalar2=-2 * PI, op0=ALU.add, op1=ALU.mult), reads=[B_rr], writes=[B_rr])
                        k.op(dve, lambda e: e.tensor_tensor(out=rr[:], in0=rr[:], in1=pre[:], op=ALU.add), reads=[B_rr, B_pre], writes=[B_rr])
                        dst, B_dst = (h1T, B_h1) if layer == 0 else (hid2T, B_hid2)
                        k.op(act, lambda e: e.activation(out=dst[0:64, cs_], in_=rr[:], func=AF.Sin), reads=[B_rr], writes=[B_dst])
            k.barrier()
            if stop_after == "A2":
                o = nc.dram_tensor("dbg_hid2T", [64, L], F32, kind="ExternalOutput").ap()
                k.dma("sp", o, hid2T[:], reads=[B_hid2], is_output=True)
                k.finish()
                return nc

            f256, B_f256 = sb(es_hy, "f256", [128, 512], F32)
            twc, B_twc = sb(es_hy, "twc", [128, 256], F32)
            tws, B_tws = sb(es_hy, "tws", [128, 256], F32)
            c64, B_c64 = sb(es_hy, "c64", [128, 128], F32)
            s64, B_s64 = sb(es_hy, "s64", [128, 128], F32)
            ns64, B_ns64 = sb(es_hy, "ns64", [128, 128], F32)
            icis, B_icis = sb(es_hy, "icis", [128, 512], F32)
            negt, B_negt = sb(es_hy, "negt", [128, 64], F32)
            selb, B_selb = sb(es_hy, "selb", [128, 32], BF16)
            for t_, n_, b_ in [(f256, "f256", B_f256), (twc, "twc", B_twc), (tws, "tws", B_tws), (c64, "c64", B_c64), (s64, "s64", B_s64),
                               (ns64, "ns64", B_ns64), (icis, "icis", B_icis), (negt, "negt", B_negt)]:
                k.dma("sp", t_[:], cn[n_], writes=[b_])
            k.dma("pool", selb[:], cn["sel"], writes=[B_selb])
            B_fc = [B_f256, B_twc, B_tws, B_c64, B_s64, B_ns64, B_icis]
            P_ext, B_P = sb(es_hy, "P_ext", [128, 66, 192], BF16)
            V, B_V = sb(es_hy, "V", [128, 64, 64], F32)
            X1, B_X1 = sb(es_hy, "X1", [128, 64, 64], BF16)
            X2, B_X2 = sb(es_hy, "X2", [128, 64, 64], BF16)
            Yout, B_Y = sb(es_hy, "Yout", [128, 64, 64], BF16)
            wg, B_wg = sb(es_hy, "wg", [128, 32, 192], BF16)
            hTb = [sb(es_hy, f"hTb{i}", [128, 32, 128], BF16) for i in range(2)]
            xt, B_xt = sb(es_hy, "xt", [128, D], F32)
            ss, B_ss = sb(es_hy, "ss", [128, 1], F32)
            rstd, B_rstd = sb(es_hy, "rstd", [128, 1], F32)
            cw, B_cw = sb(es_hy, "cw", [128, 3, 192], F32)
            cb, B_cb = sb(es_hy, "cb", [128, 192], F32)
            Hf, B_Hf = sb(es_hy, "Hf", [128, 2, 32, 64], F32)
            dec = es_hy.enter_context(nc.sbuf_tensor("dec", [128, 2, 8, 32], F32))
            B_dec = [Buf("dec0"), Buf("dec1")]
            w3g, B_w3g = sb(es_hy, "w3g", [64, 256], F32)
            drow, B_drow = sb(es_hy, "drow", [128, 64], F32)
            brow, B_brow = sb(es_hy, "browh", [1, 128], F32)
            NSET = 3
            FS = []
            for si_ in range(NSET):
                d_ = {}
                for nm_ in ("Bsb", "Kf", "Ysb"):
                    d_[nm_] = sb(es_hy, f"{nm_}{si_}", [128, 512], F32)
                d_["Esb"] = d_["Bsb"]
                d_["ETsb"] = d_["Ysb"]
                d_["tt"] = [sb(es_hy, f"tt{si_}_{i}", [128, 256], F32) for i in range(4)]
                d_["rr"] = [0]
                FS.append(d_)
            B_Vp = [Buf(f"V{i}") for i in range(32)]
            B_Yp = [Buf(f"Y{i}") for i in range(32)]
            xs = Yout[:].rearrange("p c n -> p (c n)")
            B_xs = B_Y
            tmpA = xt[:].rearrange("p (n c) -> p n c", c=64)
            tmpB = Hf[:].rearrange("p a c n -> p (a c n)").rearrange("p (n c) -> p n c", c=64)
            xv = x.rearrange("(a b) d -> b a d", b=64)

            def cmul(src, B_src, ca, cb_, B_cab, dst, B_dst_, conj, tt):
                sr, si = src[:, 0:256], src[:, 256:512]
                (t1, B1), (t2, B2), (t3, B3), (t4, B4) = tt
                k.op(dve, lambda e: e.tensor_tensor(out=t1[:], in0=sr, in1=ca, op=ALU.mult), reads=[B_src] + B_cab, writes=[B1])
                k.op(dve, lambda e: e.tensor_tensor(out=t2[:], in0=si, in1=cb_, op=ALU.mult), reads=[B_src] + B_cab, writes=[B2])
                k.op(dve, lambda e: e.tensor_tensor(out=t3[:], in0=si, in1=ca, op=ALU.mult), reads=[B_src] + B_cab, writes=[B3])
                k.op(dve, lambda e: e.tensor_tensor(out=t4[:], in0=sr, in1=cb_, op=ALU.mult), reads=[B_src] + B_cab, writes=[B4])
                k.op(pool, lambda e: e.tensor_tensor(out=dst[:, 0:256], in0=t1[:], in1=t2[:], op=(ALU.add if conj else ALU.subtract)), reads=[B1, B2], writes=[B_dst_])
                k.op(pool, lambda e: e.tensor_tensor(out=dst[:, 256:512], in0=t3[:], in1=t4[:], op=(ALU.subtract if conj else ALU.add)), reads=[B3, B4], writes=[B_dst_])

            def dft64(src, B_src, ps, B_ps, inverse, parts="both"):
                sA, sB = (ns64, s64) if inverse else (s64, ns64)
                if parts in ("both", "re"):
                    k.op(pe, lambda e: e.matmul(ps[:, 0:256], lhsT=c64[:], rhs=src[:, 0:256], start=True, stop=False), reads=[B_src, B_c64], writes=[B_ps], inc=False)
                    k.op(pe, lambda e: e.matmul(ps[:, 0:256], lhsT=sA[:], rhs=src[:, 256:512], start=False, stop=True), reads=[B_src, B_s64, B_ns64], writes=[B_ps], inc=(parts == "re"))
                if parts in ("both", "im"):
                    k.op(pe, lambda e: e.matmul(ps[:, 256:512], lhsT=c64[:], rhs=src[:, 256:512], start=True, stop=False), reads=[B_src, B_c64], writes=[B_ps], inc=False)
                    k.op(pe, lambda e: e.matmul(ps[:, 256:512], lhsT=sB[:], rhs=src[:, 0:256], start=False, stop=True), reads=[B_src, B_s64, B_ns64], writes=[B_ps])

            def sbank(fs, si_, which):
                return banks[2 * si_ + which]

            def pair_pipeline(si_, o_, p_, cp):
                fs = FS[si_]
                (Bsb, B_Bsb), (Kf, B_Kf), (Ysb, B_Ysb), (Esb, B_Esb), (ETsb, B_ETsb) = fs["Bsb"], fs["Kf"], fs["Ysb"], fs["Esb"], fs["ETsb"]
                tt = fs["tt"]
                ip = cp // 2
                psK, B_psK = sbank(fs, si_, 0)
                for d_ in range(2):
                    psA, B_psA = sbank(fs, si_, 1)
                    k.op(pe, lambda e: e.matmul(psA[:], lhsT=Hf[:, d_, 2 * p_:2 * p_ + 2, :].rearrange("p c n -> p (c n)"), rhs=f256[:], start=True, stop=True),
                         reads=[B_Hf, B_f256], writes=[B_psA])
                    yield
                    cmul(psA, B_psA, twc[:], tws[:], [B_twc, B_tws], Bsb, B_Bsb, True, tt)
                    yield
                    dft64(Bsb, B_Bsb, psK, B_psK, inverse=False, parts=("re" if d_ == 0 else "im"))
                    yield
                k.op(act, lambda e: e.copy(out=Kf[:], in_=psK[:]), reads=[B_psK], writes=[B_Kf])
                psA, B_psA = sbank(fs, si_, 0)
                k.op(pe, lambda e: e.matmul(psA[:], lhsT=V[:, cp:cp + 2, :].rearrange("p c n -> p (c n)"), rhs=f256[:], start=True, stop=True),
                     reads=[B_Vp[ip], B_f256], writes=[B_psA])
                yield
                cmul(psA, B_psA, twc[:], tws[:], [B_twc, B_tws], Bsb, B_Bsb, True, tt)
                yield
                psX, B_psX = sbank(fs, si_, 1)
                dft64(Bsb, B_Bsb, psX, B_psX, inverse=False)
                yield
                cmul(psX, B_psX, Kf[:, 0:256], Kf[:, 256:512], [B_Kf], Ysb, B_Ysb, False, tt)
                yield
                psD, B_psD = sbank(fs, si_, 0)
                dft64(Ysb, B_Ysb, psD, B_psD, inverse=True)
                yield
                cmul(psD, B_psD, twc[:], tws[:], [B_twc, B_tws], Esb, B_Esb, False, tt)
                yield
                psT, B_psT_ = sbank(fs, si_, 1)
                for h_ in range(2):
                    for ri in range(2):
                        bi = h_ * 2 + ri
                        k.op(pe, lambda e: e.transpose(psT[:, bi * 128:(bi + 1) * 128], Esb[:, ri * 256 + h_ * 128: ri * 256 + (h_ + 1) * 128], identf[:]),
                             reads=[B_Esb, B_identf], writes=[B_psT_], inc=(bi == 3))
                yield
                k.op(act, lambda e: e.copy(out=ETsb[:], in_=psT[:]), reads=[B_psT_], writes=[B_ETsb])
                yield
                psY, B_psY = sbank(fs, si_, 0)
                for bi in range(4):
                    k.op(pe, lambda e: e.matmul(psY[:, 0:128], lhsT=icis[:, bi * 128:(bi + 1) * 128], rhs=ETsb[:, bi * 128:(bi + 1) * 128], start=(bi == 0), stop=(bi == 3)),
                         reads=[B_icis, B_ETsb], writes=[B_psY], inc=(bi == 3))
                yield
                if o_ == 0:
                    k.op(dve, lambda e: e.tensor_tensor(out=V[:, cp:cp + 2, :].rearrange("p c n -> p (c n)"), in0=psY[:, 0:128],
                                                        in1=X1[:, cp:cp + 2, :].rearrange("p c n -> p (c n)"), op=ALU.mult), reads=[B_psY, B_X1], writes=[B_Vp[ip]])
                else:
                    k.op(dve, lambda e: e.tensor_tensor(out=Yout[:, cp:cp + 2, :].rearrange("p c n -> p (c n)"), in0=psY[:, 0:128],
                                                        in1=X2[:, cp:cp + 2, :].rearrange("p c n -> p (c n)"), op=ALU.mult), reads=[B_psY, B_X2], writes=[B_Yp[ip]])

            for g in range(NG):
                ch0 = g * 64
                cast_load(wg, w_hy[:, g * 192:(g + 1) * 192].rearrange("(kt p) n -> p kt n", p=128), B_wg)
                for n2 in range(64):
                    hTt, B_h = hTb[n2 % 2]
                    if g == 0:
                        k.dma("sp", xt[:], xv[n2], writes=[B_xt])
                        make_hT(None, xt[:], B_xt, hTt, B_h, 0, 1, (Yout[:].rearrange("p c n -> p (c n)"), B_Y, ss, B_ss, rstd, B_rstd))
                        k.dma("sp", hT_d[n2], hTt[:].rearrange("p k t -> p (k t)"), reads=[B_h], writes=[B_hT[n2]])
                    else:
                        k.dma("sp", hTt[:].rearrange("p k t -> p (k t)"), hT_d[n2], reads=[B_hT[n2]], writes=[B_h])
                    ps, B_ps = bank()
                    for kt in range(32):
                        k.op(pe, lambda e: e.matmul(ps[:, 0:192], lhsT=hTt[:, kt, :], rhs=wg[:, kt, :], start=(kt == 0), stop=(kt == 31)),
                             reads=[B_h, B_wg], writes=[B_ps], inc=(kt == 31))
                    k.op(act, lambda e: e.copy(out=P_ext[:, n2 + 1, :], in_=ps[:, 0:192]), reads=[B_ps], writes=[B_P])
                if stop_after == "C1" and g == 0:
                    o = nc.dram_tensor("dbg_P", [128, 66 * 192], BF16, kind="ExternalOutput").ap()
                    k.dma("sp", o, P_ext[:].rearrange("p a c -> p (a c)"), reads=[B_P], is_output=True)
                    o2 = nc.dram_tensor("dbg_hT", [2, 128, D], BF16, kind="ExternalOutput").ap()
                    for i_ in range(2):
                        k.dma("sp", hTb[i_][0][:].rearrange("p k t -> p (k t)"), hT_d[i_], reads=[B_hT[i_]], writes=[hTb[i_][1]])
                        k.dma("sp", o2[i_], hTb[i_][0][:].rearrange("p k t -> p (k t)"), reads=[hTb[i_][1]], is_output=True)
                    o3 = nc.dram_tensor("dbg_wg", [128, 32 * 192], BF16, kind="ExternalOutput").ap()
                    k.dma("sp", o3, wg[:].rearrange("p a c -> p (a c)"), reads=[B_wg], is_output=True)
                    k.finish()
                    return nc
                k.dma("sp", P_ext[1:128, 0, :], P_ext[0:127, 64, :], reads=[B_P], writes=[B_P])
                k.dma("pool", P_ext[0:1, 0, :], cn["zeros"][0:1, 0:192], writes=[B_P])
                k.dma("sp", P_ext[0:127, 65, :], P_ext[1:128, 1, :], reads=[B_P], writes=[B_P])
                k.dma("pool", P_ext[127:128, 65, :], cn["zeros"][0:1, 0:192], writes=[B_P])
                for kk in range(3):
                    k.dma("sp", cw[:, kk, :], hy_cw[kk:kk + 1, g * 192:(g + 1) * 192].partition_broadcast(128), writes=[B_cw])
                k.dma("sp", cb[:], hy_cb[0:1, g * 192:(g + 1) * 192].partition_broadcast(128), writes=[B_cb])
                for part, (dst, B_dst) in enumerate([(V, B_Vp), (X1, [B_X1]), (X2, [B_X2])]):
                    cs_ = slice(part * 64, part * 64 + 64)
                    bc = lambda kk: cw[:, kk, cs_].unsqueeze(1).broadcast_to([128, 64, 64])
                    k.op(dve, lambda e: e.tensor_tensor(out=tmpA, in0=P_ext[:, 0:64, cs_], in1=bc(0), op=ALU.mult), reads=[B_P, B_cw], writes=[B_xt])
                    k.op(pool, lambda e: e.tensor_tensor(out=tmpB, in0=P_ext[:, 1:65, cs_], in1=bc(1), op=ALU.mult), reads=[B_P, B_cw], writes=[B_Hf])
                    k.op(dve, lambda e: e.tensor_tensor(out=tmpA, in0=tmpA, in1=tmpB, op=ALU.add), reads=[B_xt, B_Hf], writes=[B_xt])
                    k.op(pool, lambda e: e.tensor_tensor(out=tmpB, in0=P_ext[:, 2:66, cs_], in1=bc(2), op=ALU.mult), reads=[B_P, B_cw], writes=[B_Hf])
                    k.op(dve, lambda e: e.tensor_tensor(out=tmpA, in0=tmpA, in1=tmpB, op=ALU.add), reads=[B_xt, B_Hf], writes=[B_xt])
                    k.op(dve, lambda e: e.tensor_tensor(out=dst[:].rearrange("p c n -> p n c"), in0=tmpA, in1=cb[:, cs_].unsqueeze(1).broadcast_to([128, 64, 64]), op=ALU.add),
                         reads=[B_xt, B_cb], writes=B_dst)
                if stop_after == "C2" and g == 0:
                    for nm, t_, b_, shp in [("dbg_V", V, B_Vp, [128, 4096])]:
                        o = nc.dram_tensor(nm, shp, F32, kind="ExternalOutput").ap()
                        k.dma("sp", o, t_[:].rearrange("p c n -> p (c n)"), reads=b_, is_output=True)
                    k.finish()
                    return nc
                k.dma("sp", w3g[:], f_w3[:, g * 256:(g + 1) * 256], writes=[B_w3g])
                k.dma("sp", drow[:], cn["deltas"][0:1, g * 64:(g + 1) * 64].partition_broadcast(128), writes=[B_drow])
                k.dma("sp", brow[:], hy_bias[0:1, g * 128:(g + 1) * 128], writes=[B_brow])
                for o_ in range(2):
                    for sbi in range(2):
                        c0 = sbi * 32
                        wc0 = (o_ * 2 + sbi) * 64
                        for blk in range(8):
                            db = blk % 2
                            for l_ in range(8):
                                n2 = blk * 8 + l_
                                k.op(act, lambda e: e.activation(out=dec[:, db, l_, :], in_=drow[:, c0:c0 + 32], func=AF.Exp, scale=negt[:, n2:n2 + 1]),
                                     reads=[B_drow, B_negt], writes=[B_dec[db]])
                            ps, B_ps = bank()
                            for l_ in range(8):
                                n2 = blk * 8 + l_
                                k.op(pe, lambda e: e.matmul(ps[:, l_ * 64:(l_ + 1) * 64], lhsT=hid2T[0:64, n2:L:64], rhs=w3g[0:64, wc0:wc0 + 64], start=True, stop=True),
                                     reads=[B_hid2, B_w3g], writes=[B_ps], inc=(l_ == 7))
                            for d_ in range(2):
                                k.op(dve, lambda e: e.tensor_tensor(out=Hf[:, d_, :, blk * 8:(blk + 1) * 8].rearrange("p c n -> p n c"),
                                                                    in0=ps[:, 0:512].rearrange("p (n d c) -> p n d c", d=2, c=32)[:, :, d_, :], in1=dec[:, db, :, :], op=ALU.mult),
                                     reads=[B_ps, B_dec[db]], writes=[B_Hf])
                        k.op(dve, lambda e: e.tensor_tensor(out=Hf[0:1, 0, :, 0], in0=Hf[0:1, 0, :, 0], in1=brow[0:1, o_ * 64 + c0: o_ * 64 + c0 + 32], op=ALU.add),
                             reads=[B_Hf, B_brow], writes=[B_Hf])
                        k.op(dve, lambda e: e.memset(Hf[0:1, 1, :, 0], 0.0), writes=[B_Hf])
                        Hf0 = Hf[:, 0, :, :].rearrange("p c n -> p (c n)")
                        Hf1 = Hf[:, 1, :, :].rearrange("p c n -> p (c n)")
                        k.op(pool, lambda e: e.tensor_tensor(out=Hf1, in0=Hf0, in1=Hf1, op=ALU.subtract), reads=[B_Hf], writes=[B_Hf])
                        k.op(pool, lambda e: e.tensor_tensor(out=Hf0, in0=Hf0, in1=Hf0, op=ALU.add), reads=[B_Hf], writes=[B_Hf])
                        k.op(pool, lambda e: e.tensor_tensor(out=Hf0, in0=Hf0, in1=Hf1, op=ALU.subtract), reads=[B_Hf], writes=[B_Hf])
                        todo = [(p_, c0 + 2 * p_) for p_ in range(16)]
                        active = []
                        for si_ in range(NSET):
                            p_, cp = todo.pop(0)
                            active.append([si_, pair_pipeline(si_, o_, p_, cp)])
                        while active:
                            for ent in list(active):
                                try:
                                    next(ent[1])
                                except StopIteration:
                                    if todo:
                                        p_, cp = todo.pop(0)
                                        ent[1] = pair_pipeline(ent[0], o_, p_, cp)
                                    else:
                                        active.remove(ent)
                ysel = P_ext[:].rearrange("p a c -> p (a c)")
                Yf = Yout[:].rearrange("p c n -> p (c n)")
                for cch in range(8):
                    ps, B_ps = bank()
                    k.op(pe, lambda e: e.matmul(ps[0:32, :], lhsT=selb[:, 0:32], rhs=Yf[:, cch * 512:(cch + 1) * 512], start=True, stop=True), reads=[B_selb] + B_Yp, writes=[B_ps])
                    k.op(act, lambda e: e.copy(out=ysel[0:32, cch * 512:(cch + 1) * 512], in_=ps[0:32, :]), reads=[B_ps], writes=[B_P])
                k.dma("sp", yT_d[ch0:ch0 + 64, :].rearrange("c (i n) -> i c n", n=64), ysel[0:32, 0:4096].rearrange("i (c n) -> i c n", n=64), reads=[B_P], writes=[B_yT])
                if stop_after == "C3" and g == 0:
                    o = nc.dram_tensor("dbg_Y", [128, 4096], F32, kind="ExternalOutput").ap()
                    k.op(dve, lambda e: e.tensor_copy(out=xt[:], in_=Yf), reads=B_Yp, writes=[B_xt])
                    k.dma("sp", o, xt[:], reads=[B_xt], is_output=True)
                    o2 = nc.dram_tensor("dbg_Z", [128, 4096], F32, kind="ExternalOutput").ap()
                    k.dma("sp", o2, V[:].rearrange("p c n -> p (c n)"), reads=B_Vp, is_output=True)
                    k.finish()
                    return nc

        k.barrier()
        NKT = 66
        NK = NKT * 128
        with ExitStack() as es_att:
            qnT, B_qnT = sb(es_att, "qnT", [128, 8, 2048], BF16)
            ss, B_ss = sb(es_att, "ss_a", [128, 2], F32)
            rstd, B_rstd = sb(es_att, "rstd_a", [128, 1], F32)
            with ExitStack() as es:
                wq, B_wq = sb(es, "wq", [128, 32, QL], BF16)
                xt, B_xt = sb(es, "xt_q", [128, D], F32)
                xs, B_xs = sb(es, "xs_q", [128, D], BF16)
                hT1, B_hT1 = sb(es, "hT1", [128, 32, 128], BF16)
                gqa, B_gqa = sb(es, "gqa", [128, QL], F32)
                qn, B_qn = sb(es, "qn", [128, QL], BF16)
                ss1, B_ss1 = sb(es, "ss1", [128, 1], F32)
                cast_load(wq, w_q.rearrange("(kt p) n -> p kt n", p=128), B_wq)
                k.dma("sp", gqa[:], g_qa.partition_broadcast(128), writes=[B_gqa])
                for m in range(16):
                    k.dma("sp", xt[:], x_own[m * 128:(m + 1) * 128, :], writes=[B_xt])
                    make_hT(None, xt[:], B_xt, hT1, B_hT1, 0, 1, (xs, B_xs, ss1, B_ss1, rstd, B_rstd))
                    pq = [bank(), bank()]
                    for hf in range(2):
                        ps, B_ps = pq[hf]
                        for kt in range(32):
                            k.op(pe, lambda e: e.matmul(ps[:], lhsT=hT1[:, kt, :], rhs=wq[:, kt, hf * 512:(hf + 1) * 512], start=(kt == 0), stop=(kt == 31)),
                                 reads=[B_hT1, B_wq], writes=[B_ps], inc=(kt == 31))
                        k.op(act, lambda e: e.activation(out=qn[:, hf * 512:(hf + 1) * 512], in_=ps[:], func=AF.Square, accum_out=ss[:, hf:hf + 1]), reads=[B_ps], writes=[B_qn, B_ss])
                    k.op(dve, lambda e: e.tensor_tensor(out=rstd[:], in0=ss[:, 0:1], in1=ss[:, 1:2], op=ALU.add), reads=[B_ss], writes=[B_rstd])
                    k.op(dve, lambda e: e.tensor_scalar(out=rstd[:], in0=rstd[:], scalar1=1.0 / QL, scalar2=EPS, op0=ALU.mult, op1=ALU.add), reads=[B_rstd], writes=[B_rstd])
                    k.op(act, lambda e: e.activation(out=rstd[:], in_=rstd[:], func=AF.Sqrt), reads=[B_rstd], writes=[B_rstd])
                    k.op(dve, lambda e: e.reciprocal(out=rstd[:], in_=rstd[:]), reads=[B_rstd], writes=[B_rstd])
                    for hf in range(2):
                        ps, B_ps = pq[hf]
                        k.op(dve, lambda e: e.scalar_tensor_tensor(out=qn[:, hf * 512:(hf + 1) * 512], in0=ps[:], scalar=rstd[:, 0:1], in1=gqa[:, hf * 512:(hf + 1) * 512],
                                                                   op0=ALU.mult, op1=ALU.mult), reads=[B_ps, B_rstd, B_gqa], writes=[B_qn])
                    for q4 in range(2):
                        for i in range(4):
                            kt = q4 * 4 + i
                            k.op(pe, lambda e: e.transpose(psT2[q4][:, i * 128:(i + 1) * 128], qn[:, kt * 128:(kt + 1) * 128], identb[:]),
                                 reads=[B_qn, B_identb], writes=[B_psT[q4]], inc=(i == 3))
                        k.op(act, lambda e: e.copy(out=qnT[:, q4 * 4:(q4 + 1) * 4, m * 128:(m + 1) * 128], in_=psT2[q4][:, 0:512].rearrange("p (k t) -> p k t", t=128)),
                             reads=[B_psT[q4]], writes=[B_qnT])
            k.barrier()
            kvnT, B_kvnT = sb(es_att, "kvnT", [128, 4, NK], BF16)
            krT, B_krT = sb(es_att, "krT", [64, NK], BF16)
            ssr, B_ssr = sb(es_att, "ssr", [128, NKT], F32)
            with ExitStack() as es:
                wkv, B_wkv = sb(es, "wkv", [128, 32, 576], BF16)
                hTk = [sb(es, f"hTk{i}", [128, 32, 128], BF16) for i in range(2)]
                xt, B_xt = sb(es, "xt_k", [128, D], F32)
                xs, B_xs = sb(es, "xs_k", [128, D], BF16)
                gkva, B_gkva = sb(es, "gkva", [128, KVL], F32)
                gkr, B_gkr = sb(es, "gkr", [128, 64], F32)
                ropek, B_ropek = sb(es, "ropek", [128, 2, 64], F32)
                kvn, B_kvn = sb(es, "kvn", [128, KVL], BF16)
                krg, B_krg = sb(es, "krg", [128, 64], F32)
                krb, B_krb = sb(es, "krb", [128, 64], BF16)
                ra, B_ra = sb(es, "ra", [128, 2, 16], F32)
                rb, B_rb = sb(es, "rb", [128, 2, 16], F32)
                ss1, B_ss1 = sb(es, "ss1k", [128, 1], F32)
                cast_load(wkv, w_kv.rearrange("(kt p) n -> p kt n", p=128), B_wkv)
                k.dma("sp", gkva[:], g_kva.partition_broadcast(128), writes=[B_gkva])
                k.dma("sp", gkr[:], gk[0:1, 128:192].partition_broadcast(128), writes=[B_gkr])
                for idx in range(NKT):
                    hTt, B_h = hTk[idx % 2]
                    if idx < 2:
                        k.dma("sp", xt[:], ctx[idx * 128:(idx + 1) * 128, :], writes=[B_xt])
                        make_hT(None, xt[:], B_xt, hTt, B_h, 2, 3, (xs, B_xs, ss1, B_ss1, rstd, B_rstd))
                    else:
                        k.dma("sp", hTt[:].rearrange("p k t -> p (k t)"), hT_d[idx - 2], reads=[B_hT[idx - 2]], writes=[B_h])
                    psA, B_psA = bank()
                    psB, B_psB = bank()
                    for kt in range(32):
                        k.op(pe, lambda e: e.matmul(psA[:], lhsT=hTt[:, kt, :], rhs=wkv[:, kt, 0:512], start=(kt == 0), stop=(kt == 31)), reads=[B_h, B_wkv], writes=[B_psA], inc=(kt == 31))
                    for kt in range(32):
                        k.op(pe, lambda e: e.matmul(psB[:, 0:64], lhsT=hTt[:, kt, :], rhs=wkv[:, kt, 512:576], start=(kt == 0), stop=(kt == 31)), reads=[B_h, B_wkv], writes=[B_psB], inc=(kt == 31))
                    k.op(act, lambda e: e.activation(out=kvn[:], in_=psA[:], func=AF.Square, accum_out=ss[:, 0:1]), reads=[B_psA], writes=[B_kvn, B_ss])
                    k.op(dve, lambda e: e.tensor_scalar(out=rstd[:], in0=ss[:, 0:1], scalar1=1.0 / KVL, scalar2=EPS, op0=ALU.mult, op1=ALU.add), reads=[B_ss], writes=[B_rstd])
                    k.op(act, lambda e: e.activation(out=rstd[:], in_=rstd[:], func=AF.Sqrt), reads=[B_rstd], writes=[B_rstd])
                    k.op(dve, lambda e: e.reciprocal(out=rstd[:], in_=rstd[:]), reads=[B_rstd], writes=[B_rstd])
                    k.op(dve, lambda e: e.scalar_tensor_tensor(out=kvn[:], in0=psA[:], scalar=rstd[:, 0:1], in1=gkva[:], op0=ALU.mult, op1=ALU.mult), reads=[B_psA, B_rstd, B_gkva], writes=[B_kvn])
                    for i in range(4):
                        k.op(pe, lambda e: e.transpose(psT2[0][:, i * 128:(i + 1) * 128], kvn[:, i * 128:(i + 1) * 128], identb[:]), reads=[B_kvn, B_identb], writes=[B_psT[0]], inc=(i == 3))
                    k.op(act, lambda e: e.copy(out=kvnT[:, :, idx * 128:(idx + 1) * 128], in_=psT2[0][:, 0:512].rearrange("p (k t) -> p k t", t=128)), reads=[B_psT[0]], writes=[B_kvnT])
                    k.op(act, lambda e: e.activation(out=krg[:], in_=psB[:, 0:64], func=AF.Square, accum_out=ssr[:, idx:idx + 1]), reads=[B_psB], writes=[B_krg, B_ssr])
                    k.op(dve, lambda e: e.tensor_tensor(out=krg[:], in0=psB[:, 0:64], in1=gkr[:], op=ALU.mult), reads=[B_psB, B_gkr], writes=[B_krg])
                    if idx < 2:
                        k.op(act, lambda e: e.copy(out=krb[:], in_=krg[:]), reads=[B_krg], writes=[B_krb])
                    else:
                        n2 = idx % 2
                        k.dma("sp", ropek[:, n2, :], cn["ropek"][:, (idx - 2) * 64:(idx - 1) * 64], writes=[B_ropek])
                        v4 = krg[:].rearrange("p (a h f) -> p a h f", a=2, h=2)
                        o4 = krb[:].rearrange("p (a h f) -> p a h f", a=2, h=2)
                        cs_ = ropek[:, n2, 0:32].rearrange("p (a f) -> p a f", a=2)
                        sn_ = ropek[:, n2, 32:64].rearrange("p (a f) -> p a f", a=2)
                        x1_, x2_ = v4[:, :, 0, :], v4[:, :, 1, :]
                        k.op(dve, lambda e: e.tensor_tensor(out=ra[:], in0=x1_, in1=cs_, op=ALU.mult), reads=[B_krg, B_ropek], writes=[B_ra])
                        k.op(dve, lambda e: e.tensor_tensor(out=rb[:], in0=x2_, in1=sn_, op=ALU.mult), reads=[B_krg, B_ropek], writes=[B_rb])
                        k.op(dve, lambda e: e.tensor_tensor(out=o4[:, :, 0, :], in0=ra[:], in1=rb[:], op=ALU.subtract), reads=[B_ra, B_rb], writes=[B_krb])
                        k.op(dve, lambda e: e.tensor_tensor(out=ra[:], in0=x1_, in1=sn_, op=ALU.mult), reads=[B_krg, B_ropek], writes=[B_ra])
                        k.op(dve, lambda e: e.tensor_tensor(out=rb[:], in0=x2_, in1=cs_, op=ALU.mult), reads=[B_krg, B_ropek], writes=[B_rb])
                        k.op(dve, lambda e: e.tensor_tensor(out=o4[:, :, 1, :], in0=ra[:], in1=rb[:], op=ALU.add), reads=[B_ra, B_rb], writes=[B_krb])
                    k.op(pe, lambda e: e.transpose(psT2[1][0:64, 0:128], krb[:], identb[:]), reads=[B_krb, B_identb], writes=[B_psT[1]])
                    k.op(act, lambda e: e.copy(out=krT[0:64, idx * 128:(idx + 1) * 128], in_=psT2[1][0:64, 0:128]), reads=[B_psT[1]], writes=[B_krT])
            k.barrier()
            with ExitStack() as es:
                bankmod[0] = 3
                bankrr[0] = 0
                (psO, B_psO), (psL, B_psL), (psSS, B_psSS) = banks[3], banks[4], banks[5]
                wqb, B_wqb = sb(es, "wqb", [128, 8, QK], BF16)
                wkvb, B_wkvb = sb(es, "wkvb", [128, 4, 256], BF16)
                KT, B_KT = sb(es, "KT", [128, NK], BF16)
                sqK, B_sqK = sb(es, "sqK", [128, 512], BF16)
                Vh, B_Vh = sb(es, "Vh", [128, NKT, 128], BF16)
                scl, B_scl = sb(es, "scl", [128, NKT], F32)
                QTn, B_QTn = sb(es, "QTn", [128, 2048], BF16)
                QTr, B_QTr = sb(es, "QTr", [64, 2048], BF16)
                gqk, B_gqk = sb(es, "gqk", [128, QK], F32)
                gk1, B_gk1 = sb(es, "gk1", [128, QK], F32)
                ropeq, B_ropeq = sb(es, "ropeq", [128, 16, 64], F32)
                qh, B_qh = sb(es, "qh", [128, QK], F32)
                qb, B_qb = sb(es, "qb", [128, QK], BF16)
                ra, B_ra = sb(es, "ra2", [128, 2, 16], F32)
                rb, B_rb = sb(es, "rb2", [128, 2, 16], F32)
                PT = [sb(es, f"PT{i}", [128, 512], BF16) for i in range(2)]
                rec, B_rec = sb(es, "rec", [128, 512], F32)
                yat, B_yat = sb(es, "yat", [128, 512], BF16)
                k.dma("sp", gqk[:], gq.partition_broadcast(128), writes=[B_gqk])
                k.dma("sp", gk1[:], gk.partition_broadcast(128), writes=[B_gk1])
                k.dma("sp", ropeq[:].rearrange("p a c -> p (a c)"), cn["ropeq"], writes=[B_ropeq])
                k.op(dve, lambda e: e.tensor_tensor(out=gqk[:, 0:128], in0=gqk[:, 0:128], in1=gk1[:, 0:128], op=ALU.mult), reads=[B_gqk, B_gk1], writes=[B_gqk])
                for h in range(NH):
                    cast_load(wqb, w_qb[:, h * QK:(h + 1) * QK].rearrange("(kt p) n -> p kt n", p=128), B_wqb)
                    cast_load(wkvb, w_kvb[:, h * 256:(h + 1) * 256].rearrange("(kt p) n -> p kt n", p=128), B_wkvb)
                    for kc in range(17):
                        w_ = 512 if kc < 16 else 256
                        c0 = kc * 512
                        ps, B_ps = bank()
                        for kt in range(4):
                            k.op(pe, lambda e: e.matmul(ps[:, 0:w_], lhsT=wkvb[:, kt, 0:128], rhs=kvnT[:, kt, c0:c0 + w_], start=(kt == 0), stop=(kt == 3)),
                                 reads=[B_wkvb, B_kvnT], writes=[B_ps], inc=(kt == 3))
                        k.op(act, lambda e: e.copy(out=KT[:, c0:c0 + w_], in_=ps[:, 0:w_]), reads=[B_ps], writes=[B_KT])
                        k.op(act, lambda e: e.activation(out=sqK[:, 0:w_], in_=ps[:, 0:w_], func=AF.Square), reads=[B_ps], writes=[B_sqK])
                        for l_ in range(w_ // 128):
                            ti = kc * 4 + l_
                            k.op(pe, lambda e: e.matmul(psSS[:, ti:ti + 1], lhsT=sqK[:, l_ * 128:(l_ + 1) * 128], rhs=onesb[:, 0:1], start=True, stop=True),
                                 reads=[B_sqK, B_onesb], writes=[B_psSS])
                    k.op(dve, lambda e: e.scalar_tensor_tensor(out=scl[:], in0=psSS[:, 0:NKT], scalar=QK * EPS, in1=ssr[:], op0=ALU.add, op1=ALU.add), reads=[B_psSS, B_ssr], writes=[B_scl])
                    k.op(act, lambda e: e.activation(out=scl[:], in_=scl[:], func=AF.Sqrt), reads=[B_scl], writes=[B_scl])
                    k.op(dve, lambda e: e.reciprocal(out=scl[:], in_=scl[:]), reads=[B_scl], writes=[B_scl])
                    for vt in range(0, NKT, 4):
                        nl = min(4, NKT - vt)
                        ps, B_ps = bank()
                        for l_ in range(nl):
                            for kt in range(4):
                                k.op(pe, lambda e: e.matmul(ps[:, l_ * 128:(l_ + 1) * 128], lhsT=kvnT[:, kt, (vt + l_) * 128:(vt + l_ + 1) * 128], rhs=wkvb[:, kt, 128:256],
                                                            start=(kt == 0), stop=(kt == 3)), reads=[B_kvnT, B_wkvb], writes=[B_ps], inc=(kt == 3 and l_ == nl - 1))
                        k.op(act, lambda e: e.copy(out=Vh[:, vt:vt + nl, :], in_=ps[:, 0:nl * 128].rearrange("p (a d) -> p a d", d=128)), reads=[B_ps], writes=[B_Vh])
                    for m in range(16):
                        ps, B_ps = bank()
                        for kt in range(8):
                            k.op(pe, lambda e: e.matmul(ps[:, 0:QK], lhsT=qnT[:, kt, m * 128:(m + 1) * 128], rhs=wqb[:, kt, :], start=(kt == 0), stop=(kt == 7)),
                                 reads=[B_qnT, B_wqb], writes=[B_ps], inc=(kt == 7))
                        k.op(act, lambda e: e.activation(out=qh[:], in_=ps[:, 0:QK], func=AF.Square, accum_out=ss[:, 0:1]), reads=[B_ps], writes=[B_qh, B_ss])
                        k.op(dve, lambda e: e.tensor_scalar(out=rstd[:], in0=ss[:, 0:1], scalar1=1.0 / QK, scalar2=EPS, op0=ALU.mult, op1=ALU.add), reads=[B_ss], writes=[B_rstd])
                        k.op(act, lambda e: e.activation(out=rstd[:], in_=rstd[:], func=AF.Sqrt), reads=[B_rstd], writes=[B_rstd])
                        k.op(dve, lambda e: e.reciprocal(out=rstd[:], in_=rstd[:]), reads=[B_rstd], writes=[B_rstd])
                        k.op(dve, lambda e: e.scalar_tensor_tensor(out=qh[:], in0=ps[:, 0:QK], scalar=rstd[:, 0:1], in1=gqk[:], op0=ALU.mult, op1=ALU.mult), reads=[B_ps, B_rstd, B_gqk], writes=[B_qh])
                        k.op(act, lambda e: e.copy(out=qb[:, 0:128], in_=qh[:, 0:128]), reads=[B_qh], writes=[B_qb])
                        v4 = qh[:, 128:192].rearrange("p (a h f) -> p a h f", a=2, h=2)
                        o4 = qb[:, 128:192].rearrange("p (a h f) -> p a h f", a=2, h=2)
                        cs_ = ropeq[:, m, 0:32].rearrange("p (a f) -> p a f", a=2)
                        sn_ = ropeq[:, m, 32:64].rearrange("p (a f) -> p a f", a=2)
                        x1_, x2_ = v4[:, :, 0, :], v4[:, :, 1, :]
                        k.op(dve, lambda e: e.tensor_tensor(out=ra[:], in0=x1_, in1=cs_, op=ALU.mult), reads=[B_qh, B_ropeq], writes=[B_ra])
                        k.op(dve, lambda e: e.tensor_tensor(out=rb[:], in0=x2_, in1=sn_, op=ALU.mult), reads=[B_qh, B_ropeq], writes=[B_rb])
                        k.op(dve, lambda e: e.tensor_tensor(out=o4[:, :, 0, :], in0=ra[:], in1=rb[:], op=ALU.subtract), reads=[B_ra, B_rb], writes=[B_qb])
                        k.op(dve, lambda e: e.tensor_tensor(out=ra[:], in0=x1_, in1=sn_, op=ALU.mult), reads=[B_qh, B_ropeq], writes=[B_ra])
                        k.op(dve, lambda e: e.tensor_tensor(out=rb[:], in0=x2_, in1=cs_, op=ALU.mult), reads=[B_qh, B_ropeq], writes=[B_rb])
                        k.op(dve, lambda e: e.tensor_tensor(out=o4[:, :, 1, :], in0=ra[:], in1=rb[:], op=ALU.add), reads=[B_ra, B_rb], writes=[B_qb])
                        hb = m % 2
                        k.op(pe, lambda e: e.transpose(psT2[hb][:, 0:128], qb[:, 0:128], identb[:]), reads=[B_qb, B_identb], writes=[B_psT[hb]], inc=False)
                        k.op(pe, lambda e: e.transpose(psT2[hb][0:64, 128:256], qb[:, 128:192], identb[:]), reads=[B_qb, B_identb], writes=[B_psT[hb]])
                        k.op(act, lambda e: e.copy(out=QTn[:, m * 128:(m + 1) * 128], in_=psT2[hb][:, 0:128]), reads=[B_psT[hb]], writes=[B_QTn])
                        k.op(act, lambda e: e.copy(out=QTr[0:64, m * 128:(m + 1) * 128], in_=psT2[hb][0:64, 128:256]), reads=[B_psT[hb]], writes=[B_QTr])
                    for qc in range(4):
                        qs = slice(qc * 512, (qc + 1) * 512)
                        for kt in range(NKT):
                            ps, B_ps = bank()
                            pt, B_pt = PT[kt % 2]
                            k.op(pe, lambda e: e.matmul(ps[:], lhsT=KT[:, kt * 128:(kt + 1) * 128], rhs=QTn[:, qs], start=True, stop=False), reads=[B_KT, B_QTn], writes=[B_ps], inc=False)
                            k.op(pe, lambda e: e.matmul(ps[:], lhsT=krT[0:64, kt * 128:(kt + 1) * 128], rhs=QTr[0:64, qs], start=False, stop=True), reads=[B_krT, B_QTr], writes=[B_ps])
                            k.op(act, lambda e: e.activation(out=pt[:], in_=ps[:], func=AF.Exp, scale=scl[:, kt:kt + 1]), reads=[B_ps, B_scl], writes=[B_pt])
                            k.op(pe, lambda e: e.matmul(psO[:], lhsT=Vh[:, kt, :], rhs=pt[:], start=(kt == 0), stop=(kt == NKT - 1)), reads=[B_Vh, B_pt], writes=[B_psO], inc=False)
                            k.op(pe, lambda e: e.matmul(psL[:], lhsT=onesb[:], rhs=pt[:], start=(kt == 0), stop=(kt == NKT - 1)), reads=[B_onesb, B_pt], writes=[B_psL])
                        k.op(dve, lambda e: e.reciprocal(out=rec[:], in_=psL[:]), reads=[B_psL], writes=[B_rec])
                        k.op(dve, lambda e: e.tensor_tensor(out=yat[:], in0=psO[:], in1=rec[:], op=ALU.mult), reads=[B_psO, B_rec], writes=[B_yat])
                        k.dma("sp", yT_d[HYW + h * 128: HYW + (h + 1) * 128, qs], yat[:], reads=[B_yat], writes=[B_yT])
                bankmod[0] = 6
                bankrr[0] = 0
        k.barrier()
        if stop_after == "E":
            o = nc.dram_tensor("dbg_yT", [D, 2048], BF16, kind="ExternalOutput").ap()
            with ExitStack() as es:
                t_, B_t = sb(es, "dbg_t", [128, 32, 2048], BF16)
                k.dma("sp", t_[:], yT_d.rearrange("(k p) t -> p k t", p=128), reads=[B_yT], writes=[B_t])
                k.dma("sp", o.rearrange("(k p) t -> p k t", p=128), t_[:], reads=[B_t], is_output=True)
            k.finish()
            return nc

        with ExitStack() as es:
            g1row, B_g1 = sb(es, "g1row", [128, D], F32)
            yTs, B_yTs = sb(es, "yTs", [128, 32, 1024], BF16)
            wo = [sb(es, f"wo{i}", [128, 32, 512], BF16) for i in range(2)]
            xin = [sb(es, f"xin{i}", [128, 512], F32) for i in range(2)]
            xo = [sb(es, f"xo{i}", [128, 512], F32) for i in range(2)]
            k.dma("sp", g1row[:], mod_d[0:1, 2 * D:3 * D].partition_broadcast(128), reads=[B_mod], writes=[B_g1])
            cnt = 0
            for half in range(2):
                for kt in range(32):
                    k.dma("sp", yTs[:, kt, :], yT_d[kt * 128:(kt + 1) * 128, half * 1024:(half + 1) * 1024], reads=[B_yT], writes=[B_yTs])
                for f_ in range(8):
                    wt, B_wt = wo[f_ % 2]
                    fs = slice(f_ * 512, (f_ + 1) * 512)
                    cast_load(wt, w_out[:, fs].rearrange("(kt p) n -> p kt n", p=128), B_wt)
                    for tt_ in range(8):
                        r0 = half * 1024 + tt_ * 128
                        xi, B_xi = xin[cnt % 2]
                        xo_, B_xo = xo[cnt % 2]
                        cnt += 1
                        k.dma("sp", xi[:], x_own[r0:r0 + 128, fs], writes=[B_xi])
                        ps, B_ps = bank()
                        for kt in range(32):
                            k.op(pe, lambda e: e.matmul(ps[:], lhsT=yTs[:, kt, tt_ * 128:(tt_ + 1) * 128], rhs=wt[:, kt, :], start=(kt == 0), stop=(kt == 31)),
                                 reads=[B_yTs, B_wt], writes=[B_ps], inc=(kt == 31))
                        k.op(dve, lambda e: e.tensor_tensor(out=xo_[:], in0=ps[:], in1=g1row[:, fs], op=ALU.mult), reads=[B_ps, B_g1], writes=[B_xo])
                        k.op(pool, lambda e: e.tensor_tensor(out=xo_[:], in0=xo_[:], in1=xi[:], op=ALU.add), reads=[B_xo, B_xi], writes=[B_xo])
                        k.dma("sp", xnew_d[r0:r0 + 128, fs], xo_[:], reads=[B_xo], writes=[B_xnew])
        k.barrier()
        if stop_after == "F":
            for i_ in range(16):
                k.dma("sp", out[i_ * 128:(i_ + 1) * 128, :], xnew_d[i_ * 128:(i_ + 1) * 128, :], reads=[B_xnew], is_output=True)
            k.finish()
            return nc

        with ExitStack() as es:
            h2T, B_h2T = sb(es, "h2T", [128, 32, 512], BF16)
            acc = es.enter_context(nc.sbuf_tensor("acc", [128, 4, D], F32))
            B_acc = [Buf(f"acc{i}") for i in range(4)]
            aT, B_aT = sb(es, "aT", [128, 16, 512], BF16)
            w1t = [sb(es, f"w1t{i}", [128, 32, 256], BF16) for i in range(2)]
            w2c = [sb(es, f"w2c{i}", [128, 16, 256], BF16) for i in range(2)]
            xs, B_xs = sb(es, "xs_m", [128, D], BF16)
            rl, B_rl = sb(es, "rl", [128, 512], F32)
            ss1, B_ss1 = sb(es, "ss1m", [128, 1], F32)
            rstd, B_rstd = sb(es, "rstd_m", [128, 1], F32)
            g2p, B_g2p = sb(es, "g2p", [128, 512], F32)
            xin, B_xin = sb(es, "xin_m", [128, 512], F32)
            oo, B_oo = sb(es, "oo", [128, 512], F32)
            n1 = 0
            n2_ = 0
            for ck in range(4):
                for tt_ in range(4):
                    r0 = ck * 512 + tt_ * 128
                    k.dma("sp", acc[:, tt_, :], xnew_d[r0:r0 + 128, :], reads=[B_xnew], writes=[B_acc[tt_]])
                    make_hT(None, acc[:, tt_, :], B_acc[tt_], h2T[:, :, tt_ * 128:(tt_ + 1) * 128], B_h2T, 4, 5, (xs, B_xs, ss1, B_ss1, rstd, B_rstd))
                for fc in range(8):
                    for ft in range(16):
                        if ft % 2 == 0:
                            ff = fc * 2048 + ft * 128
                            wt, B_wt = w1t[n1 % 2]
                            n1 += 1
                            cast_load(wt, w_mlp1[:, ff:ff + 256].rearrange("(kt p) n -> p kt n", p=128), B_wt, step=8)
                        fsub = slice((ft % 2) * 128, (ft % 2) * 128 + 128)
                        ps, B_ps = bank()
                        for kt in range(32):
                            k.op(pe, lambda e: e.matmul(ps[:], lhsT=wt[:, kt, fsub], rhs=h2T[:, kt, :], start=(kt == 0), stop=(kt == 31)), reads=[B_wt, B_h2T], writes=[B_ps], inc=(kt == 31))
                        k.op(act, lambda e: e.activation(out=rl[:], in_=ps[:], func=AF.Relu), reads=[B_ps], writes=[B_rl])
                        k.op(dve, lambda e: e.tensor_tensor(out=aT[:, ft, :], in0=rl[:], in1=rl[:], op=ALU.mult), reads=[B_rl], writes=[B_aT])
                    for fo in range(16):
                        wt, B_wt = w2c[n2_ % 2]
                        n2_ += 1
                        os_ = slice(fo * 256, (fo + 1) * 256)
                        cast_load(wt, w_mlp2[fc * 2048:(fc + 1) * 2048, os_].rearrange("(ft p) n -> p ft n", p=128), B_wt, step=8)
                        for tt_ in range(4):
                            ps, B_ps = bank()
                            for ft in range(16):
                                k.op(pe, lambda e: e.matmul(ps[:, 0:256], lhsT=aT[:, ft, tt_ * 128:(tt_ + 1) * 128], rhs=wt[:, ft, :], start=(ft == 0), stop=(ft == 15)),
                                     reads=[B_aT, B_wt], writes=[B_ps], inc=(ft == 15))
                            if fc == 0:
                                k.op(act, lambda e: e.copy(out=acc[:, tt_, os_], in_=ps[:, 0:256]), reads=[B_ps], writes=[B_acc[tt_]])
                            else:
                                k.op(dve, lambda e: e.tensor_tensor(out=acc[:, tt_, os_], in0=acc[:, tt_, os_], in1=ps[:, 0:256], op=ALU.add), reads=[B_ps, B_acc[tt_]], writes=[B_acc[tt_]])
                for f8 in range(8):
                    fs = slice(f8 * 512, (f8 + 1) * 512)
                    k.dma("sp", g2p[:], mod_d[0:1, 5 * D + f8 * 512: 5 * D + (f8 + 1) * 512].partition_broadcast(128), reads=[B_mod], writes=[B_g2p])
                    for tt_ in range(4):
                        r0 = ck * 512 + tt_ * 128
                        k.dma("sp", xin[:], xnew_d[r0:r0 + 128, fs], reads=[B_xnew], writes=[B_xin])
                        k.op(dve, lambda e: e.tensor_tensor(out=oo[:], in0=acc[:, tt_, fs], in1=g2p[:], op=ALU.mult), reads=[B_acc[tt_], B_g2p], writes=[B_oo])
                        k.op(pool, lambda e: e.tensor_tensor(out=oo[:], in0=oo[:], in1=xin[:], op=ALU.add), reads=[B_oo, B_xin], writes=[B_oo])
                        k.dma("sp", out[r0:r0 + 128, fs], oo[:], reads=[B_oo], is_output=True)
        k.finish()
        return nc


def _chan_groups(j, NG):
    if NG == NG_ALL:
        return [np.arange(g * 64, (g + 1) * 64) for g in range(NG)]
    return [np.arange(512 * j + g * 64, 512 * j + (g + 1) * 64) for g in range(NG)]


def make_in_map(inp, core, NG=NG_ALL):
    b, j = core // 4, core % 4
    f = lambda a: np.ascontiguousarray(np.asarray(a, dtype=np.float32))
    groups = _chan_groups(j, NG)
    w_in = np.asarray(inp["w_in"])[0]
    hcols = np.concatenate([np.concatenate([g, HYW + g, 2 * HYW + g]) for g in groups])
    w3 = np.asarray(inp["hy_filt_w3"])[0].reshape(64, 2, 2, HYW)
    w3g = np.concatenate([w3[:, :, :, g].reshape(64, 2, 2, 2, 32).transpose(0, 2, 3, 1, 4).reshape(64, 256) for g in groups], 1)
    hb = np.asarray(inp["hy_bias"])[0]
    hbg = np.concatenate([hb[:, g].reshape(-1) for g in groups])[None, :]
    m = {
        "x": f(inp["x"][b]), "x_own": f(inp["x"][b][2048 * j:2048 * (j + 1)]), "ctx": f(inp["ctx"][b]),
        "cvec": f(np.stack([np.asarray(inp["c"])[b], np.asarray(inp["c_ctx"])])),
        "norm_g": f(np.stack([np.asarray(inp["norm1_g"])[0], np.asarray(inp["norm2_g"])[0]])),
        "w_ada": f(inp["w_ada"][0]), "b_ada": f(np.asarray(inp["b_ada"])[0][None, :]),
        "w_hy": f(w_in[:, hcols]), "w_q": f(w_in[:, HYC:HYC + QL]), "w_kv": f(w_in[:, HYC + QL:]),
        "hy_cw": f(np.asarray(inp["hy_conv_w"])[0][:, hcols]), "hy_cb": f(np.asarray(inp["hy_conv_b"])[0][hcols][None, :]),
        "f_w1": f(inp["hy_filt_w1"][0]), "f_b1": f(np.asarray(inp["hy_filt_b1"])[0][:, None]), "f_w2": f(inp["hy_filt_w2"][0]),
        "f_b2": f(np.asarray(inp["hy_filt_b2"])[0][:, None]), "f_freq": f(np.asarray(inp["hy_freq"])[0][:, None]),
        "f_w3": f(w3g), "hy_bias": f(hbg),
        "g_qa": f(np.asarray(inp["mla_g_qa"])[0][None, :]), "g_kva": f(np.asarray(inp["mla_g_kva"])[0][None, :]),
        "w_qb": f(inp["mla_w_qb"][0]), "w_kvb": f(inp["mla_w_kvb"][0]),
        "gq": f(np.asarray(inp["mla_q_norm_g"])[0][None, :]), "gk": f(np.asarray(inp["mla_k_norm_g"])[0][None, :]),
        "w_out": f(inp["w_out"][0]), "w_mlp1": f(inp["w_mlp1"][0]), "w_mlp2": f(inp["w_mlp2"][0]),
    }
    c = _consts(j)
    c["deltas"] = np.ascontiguousarray(np.concatenate([c["deltas"][0, g] for g in groups])[None, :])
    for kname, v in c.items():
        m["c_" + kname] = np.ascontiguousarray(v.astype(np.float32))
    return m


def kernel(**inputs):
    nc = build_program()
    in_maps = [make_in_map(inputs, core) for core in range(8)]
    res = run_bass_kernel_spmd(nc, in_maps, core_ids=list(range(8)))
    outp = np.zeros((NB, L, D), np.float32)
    for core in range(8):
        b, j = core // 4, core % 4
        outp[b, 2048 * j:2048 * (j + 1)] = res.results[core]["out"]
    return outp
```
